# Optimizing a Trainium2 kernel written in Bass

```python
import math
import jax, jax.numpy as jnp
from jax import lax
import numpy as np

D_MODEL = 4096
BATCH = 2
SEQ = 4096
DEPTH = 1

MEM_LEN = 256
RWKV_WIDTH = D_MODEL // 2
RWKV_HEAD = 64
RWKV_HEADS = RWKV_WIDTH // RWKV_HEAD
RWKV_DECAY_LORA = 96
RWKV_A_LORA = 96
RWKV_GATE_LORA = 256
RWKV_GN_EPS = 64e-5
RWKV_COLS = 3 * RWKV_WIDTH + RWKV_DECAY_LORA + RWKV_A_LORA + RWKV_GATE_LORA
GDN_WIDTH = D_MODEL // 2
GDN_HEAD = 128
GDN_HEADS = GDN_WIDTH // GDN_HEAD
GDN_CONV = 4
GDN_CHUNK = 64
GDN_COLS = 4 * GDN_WIDTH + 2 * GDN_HEADS
N_IN = RWKV_COLS + GDN_COLS + 2 * D_MODEL
XA_HEADS = 4
XA_HEAD = 128
XA_WIDTH = XA_HEADS * XA_HEAD
D_FF = 4 * D_MODEL
NORM_EPS = 1e-6
L2_EPS = 1e-6

kernel_name = 'hybrid_rwkv7_gdn_memxattn_layer'


def rms_norm(x, gain, eps=NORM_EPS):
    xf = x.astype(jnp.float32)
    y = xf * lax.rsqrt(jnp.mean(xf * xf, axis=-1, keepdims=True) + eps)
    return (y * gain.astype(jnp.float32)).astype(x.dtype)


def l2_normalize(x, eps=L2_EPS):
    xf = x.astype(jnp.float32)
    return xf * lax.rsqrt(jnp.sum(xf * xf, axis=-1, keepdims=True) + eps)


def causal_depthwise_conv(x, w):
    return lax.conv_general_dilated(
        x, w[:, None, :].astype(x.dtype), window_strides=(1,),
        padding=[(w.shape[0] - 1, 0)], dimension_numbers=('NWC', 'WIO', 'NWC'),
        feature_group_count=x.shape[-1])


def rwkv7_recurrence(r, log_w, k, v, a, b):
    def step(S, inp):
        r_t, lw_t, k_t, v_t, a_t, b_t = inp
        sa = jnp.einsum('bhvk,bhk->bhv', S, a_t)
        S = (S * jnp.exp(lw_t)[:, :, None, :] + sa[..., :, None] * b_t[..., None, :]
             + v_t[..., :, None] * k_t[..., None, :])
        return S, jnp.einsum('bhvk,bhk->bhv', S, r_t)
    B, T, H, N = r.shape
    xs = tuple(jnp.moveaxis(t, 1, 0) for t in (r, log_w, k, v, a, b))
    _, y = lax.scan(step, jnp.zeros((B, H, N, N), jnp.float32), xs)
    return jnp.moveaxis(y, 0, 1)


def rwkv7_time_mix(p_rw, shift_mix, w0, w_up, a0, a_up, g_up, k_k, k_a, r_k, gn_w, gn_b):
    B, T, _ = p_rw.shape
    dt = p_rw.dtype
    prev = jnp.pad(p_rw, ((0, 0), (1, 0), (0, 0)))[:, :-1]
    xm = p_rw + (prev - p_rw) * shift_mix
    o1 = RWKV_WIDTH
    o2 = o1 + RWKV_WIDTH
    o3 = o2 + RWKV_WIDTH
    o4 = o3 + RWKV_DECAY_LORA
    o5 = o4 + RWKV_A_LORA
    r, k, v, lw, la, lg = jnp.split(xm, [o1, o2, o3, o4, o5], axis=-1)
    w_raw = -jax.nn.softplus(-(w0 + jnp.tanh(lw) @ w_up)) - 0.5
    log_decay = -jnp.exp(w_raw.astype(jnp.float32))
    a = jax.nn.sigmoid(a0 + la @ a_up)
    gate = jax.nn.sigmoid(lg) @ g_up
    heads = lambda t: t.reshape(B, T, RWKV_HEADS, RWKV_HEAD)
    kk = l2_normalize(heads(k * k_k))
    k = k * (1.0 + (a - 1.0) * k_a)
    r_h, k_h, v_h, a_h = heads(r), heads(k), heads(v), heads(a).astype(jnp.float32)
    f32 = lambda t: t.astype(jnp.float32)
    y = rwkv7_recurrence(f32(r_h), heads(log_decay), f32(k_h), f32(v_h), -kk, kk * a_h)
    mean = jnp.mean(y, axis=-1, keepdims=True)
    var = jnp.mean(jnp.square(y - mean), axis=-1, keepdims=True)
    y = ((y - mean) * lax.rsqrt(var + RWKV_GN_EPS)).reshape(B, T, RWKV_WIDTH)
    y = y * gn_w.astype(jnp.float32) + gn_b.astype(jnp.float32)
    bonus = jnp.sum(r_h * k_h * r_k, axis=-1, keepdims=True) * v_h
    y = y + bonus.reshape(B, T, RWKV_WIDTH).astype(jnp.float32)
    return y.astype(dt) * gate


def gated_delta_chunked(q, k, v, g, beta):
    B, T, H, K = q.shape
    V = v.shape[-1]
    C = GDN_CHUNK
    n = T // C

    def chunks(t):
        return jnp.moveaxis(t.reshape((B, n, C, H) + t.shape[3:]), 3, 1)

    q = chunks(q) * (K ** -0.5)
    k, v, g, beta = chunks(k), chunks(v), chunks(g), chunks(beta)
    gc = jnp.cumsum(g, axis=-1)
    idx = jnp.arange(C)
    causal = idx[:, None] >= idx[None, :]
    strict = idx[:, None] > idx[None, :]
    decay = jnp.where(causal, jnp.exp(jnp.where(causal, gc[..., :, None] - gc[..., None, :], 0.0)), 0.0)
    kb = k * beta[..., None]
    L = jnp.where(strict, jnp.einsum('bhnik,bhnjk->bhnij', kb, k) * decay, 0.0)
    eye = jnp.eye(C, dtype=q.dtype)
    tinv = lax.linalg.triangular_solve(L + eye, jnp.broadcast_to(eye, L.shape),
                                       left_side=True, lower=True, unit_diagonal=True)
    u = jnp.einsum('bhnij,bhnjv->bhniv', tinv, v * beta[..., None])
    w = jnp.einsum('bhnij,bhnjk->bhnik', tinv, kb * jnp.exp(gc)[..., None])
    a_intra = jnp.where(causal, jnp.einsum('bhnik,bhnjk->bhnij', q, k) * decay, 0.0)
    q_dec = q * jnp.exp(gc)[..., None]
    k_dec = k * jnp.exp(gc[..., -1:] - gc)[..., None]
    g_last = jnp.exp(gc[..., -1])

    def step(S, inp):
        qd, kd, u_i, w_i, a_i, gl = inp
        v_new = u_i - jnp.einsum('bhck,bhkv->bhcv', w_i, S)
        o = jnp.einsum('bhck,bhkv->bhcv', qd, S) + jnp.einsum('bhcj,bhjv->bhcv', a_i, v_new)
        S = S * gl[..., None, None] + jnp.einsum('bhck,bhcv->bhkv', kd, v_new)
        return S, o

    xs = tuple(jnp.moveaxis(t, 2, 0) for t in (q_dec, k_dec, u, w, a_intra, g_last))
    _, o = lax.scan(step, jnp.zeros((B, H, K, V), jnp.float32), xs)
    o = jnp.moveaxis(o, 0, 2)
    return jnp.moveaxis(o, 1, 3).reshape(B, T, H, V)


def gated_deltanet_mix(p_qkv, p_z, p_beta, p_alpha, conv_w, a_log, dt_bias, norm_w):
    B, T, _ = p_qkv.shape
    dt = p_qkv.dtype
    qkv = jax.nn.silu(causal_depthwise_conv(p_qkv, conv_w))
    q, k, v = jnp.split(qkv, 3, axis=-1)
    heads = lambda t: t.reshape(B, T, GDN_HEADS, GDN_HEAD)
    q = l2_normalize(heads(q))
    k = l2_normalize(heads(k))
    v = heads(v).astype(jnp.float32)
    beta = jax.nn.sigmoid(p_beta.astype(jnp.float32))
    g = -jnp.exp(a_log.astype(jnp.float32)) * jax.nn.softplus(
        p_alpha.astype(jnp.float32) + dt_bias.astype(jnp.float32))
    o = gated_delta_chunked(q, k, v, g, beta)
    o = o * lax.rsqrt(jnp.mean(o * o, axis=-1, keepdims=True) + NORM_EPS) * norm_w.astype(jnp.float32)
    o = o * jax.nn.silu(heads(p_z).astype(jnp.float32))
    return o.reshape(B, T, GDN_WIDTH).astype(dt)


def hybrid_mixer(u, w_in, shift_mix, w0, w_up, a0, a_up, g_up, k_k, k_a, r_k, gn_w, gn_b,
                 conv_w, a_log, dt_bias, gdn_norm_w, w_br_rwkv, w_br_gdn, w_out):
    p = u @ w_in
    o1 = RWKV_COLS
    o2 = o1 + 3 * GDN_WIDTH
    o3 = o2 + GDN_WIDTH
    o4 = o3 + GDN_HEADS
    o5 = o4 + GDN_HEADS
    o6 = o5 + D_MODEL
    p_rw, p_qkv, p_z, p_beta, p_alpha, p_gate_rw, p_gate_gdn = jnp.split(
        p, [o1, o2, o3, o4, o5, o6], axis=-1)
    y_rw = rwkv7_time_mix(p_rw, shift_mix, w0, w_up, a0, a_up, g_up, k_k, k_a, r_k, gn_w, gn_b)
    y_gdn = gated_deltanet_mix(p_qkv, p_z, p_beta, p_alpha, conv_w, a_log, dt_bias, gdn_norm_w)
    merged = (jax.nn.sigmoid(p_gate_rw) * (y_rw @ w_br_rwkv)
              + jax.nn.sigmoid(p_gate_gdn) * (y_gdn @ w_br_gdn))
    return merged @ w_out


def memory_cross_attention(c, m, w_q, w_kv, w_o):
    B, T, _ = c.shape
    M = m.shape[1]
    q = (c @ w_q).reshape(B, T, XA_HEADS, XA_HEAD)
    k, v = jnp.split(m @ w_kv, 2, axis=-1)
    k = k.reshape(B, M, XA_HEADS, XA_HEAD)
    v = v.reshape(B, M, XA_HEADS, XA_HEAD)
    s = jnp.einsum('bthd,bmhd->bhtm', q, k).astype(jnp.float32) * (XA_HEAD ** -0.5)
    pr = jax.nn.softmax(s, axis=-1).astype(v.dtype)
    o = jnp.einsum('bhtm,bmhd->bthd', pr, v).reshape(B, T, XA_WIDTH)
    return o @ w_o


def squared_relu_mlp(f, w_up, w_down):
    return jnp.square(jax.nn.relu(f @ w_up)) @ w_down


def setup_inputs(seed: int = 0) -> dict:
    key = jax.random.key(seed)
    ks = iter(jax.random.split(key, 48))
    Ld = DEPTH

    def nrm(shape, scale):
        return scale * jax.random.normal(next(ks), shape, jnp.float32)

    def unif(shape, lo, hi):
        return jax.random.uniform(next(ks), shape, jnp.float32, lo, hi)

    def gain(width):
        return 1.0 + nrm((Ld, width), 0.02)

    dt = jnp.exp(unif((Ld, GDN_HEADS), math.log(1e-3), math.log(1e-1)))
    dt_bias = dt + jnp.log(-jnp.expm1(-dt))
    return {
        'x': nrm((BATCH, SEQ, D_MODEL), 1.0),
        'mem': nrm((BATCH, MEM_LEN, D_MODEL), 1.0),
        'mix_norm_pre': gain(D_MODEL),
        'mix_norm_post': gain(D_MODEL),
        'w_in': nrm((Ld, D_MODEL, N_IN), D_MODEL ** -0.5),
        'rwkv_shift_mix': unif((Ld, RWKV_COLS), 0.0, 1.0),
        'rwkv_w0': unif((Ld, RWKV_WIDTH), -6.0, -1.0),
        'rwkv_w_up': nrm((Ld, RWKV_DECAY_LORA, RWKV_WIDTH), 0.5 * RWKV_DECAY_LORA ** -0.5),
        'rwkv_a0': nrm((Ld, RWKV_WIDTH), 0.1),
        'rwkv_a_up': nrm((Ld, RWKV_A_LORA, RWKV_WIDTH), RWKV_A_LORA ** -0.5),
        'rwkv_g_up': nrm((Ld, RWKV_GATE_LORA, RWKV_WIDTH), RWKV_GATE_LORA ** -0.5),
        'rwkv_k_k': 0.85 + nrm((Ld, RWKV_WIDTH), 0.02),
        'rwkv_k_a': 1.0 + nrm((Ld, RWKV_WIDTH), 0.02),
        'rwkv_r_k': -0.04 + nrm((Ld, RWKV_HEADS, RWKV_HEAD), 0.05),
        'rwkv_gn_w': gain(RWKV_WIDTH),
        'rwkv_gn_b': nrm((Ld, RWKV_WIDTH), 0.02),
        'gdn_conv_w': nrm((Ld, GDN_CONV, 3 * GDN_WIDTH), GDN_CONV ** -0.5),
        'gdn_a_log': jnp.log(unif((Ld, GDN_HEADS), 1.0, 16.0)),
        'gdn_dt_bias': dt_bias,
        'gdn_norm_w': gain(GDN_HEAD),
        'w_branch_rwkv': nrm((Ld, RWKV_WIDTH, D_MODEL), RWKV_WIDTH ** -0.5),
        'w_branch_gdn': nrm((Ld, GDN_WIDTH, D_MODEL), GDN_WIDTH ** -0.5),
        'w_mix_out': nrm((Ld, D_MODEL, D_MODEL), D_MODEL ** -0.5),
        'xa_norm_pre': gain(D_MODEL),
        'xa_norm_mem': gain(D_MODEL),
        'xa_norm_post': gain(D_MODEL),
        'xa_w_q': nrm((Ld, D_MODEL, XA_WIDTH), D_MODEL ** -0.5),
        'xa_w_kv': nrm((Ld, D_MODEL, 2 * XA_WIDTH), D_MODEL ** -0.5),
        'xa_w_o': nrm((Ld, XA_WIDTH, D_MODEL), XA_WIDTH ** -0.5),
        'mlp_norm_pre': gain(D_MODEL),
        'mlp_norm_post': gain(D_MODEL),
        'mlp_w_up': nrm((Ld, D_MODEL, D_FF), D_MODEL ** -0.5),
        'mlp_w_down': nrm((Ld, D_FF, D_MODEL), D_FF ** -0.5),
    }


def reference(x, mem, mix_norm_pre, mix_norm_post, w_in,
              rwkv_shift_mix, rwkv_w0, rwkv_w_up, rwkv_a0, rwkv_a_up, rwkv_g_up,
              rwkv_k_k, rwkv_k_a, rwkv_r_k, rwkv_gn_w, rwkv_gn_b,
              gdn_conv_w, gdn_a_log, gdn_dt_bias, gdn_norm_w,
              w_branch_rwkv, w_branch_gdn, w_mix_out,
              xa_norm_pre, xa_norm_mem, xa_norm_post, xa_w_q, xa_w_kv, xa_w_o,
              mlp_norm_pre, mlp_norm_post, mlp_w_up, mlp_w_down):
    h = x
    for l in range(DEPTH):
        u = rms_norm(h, mix_norm_pre[l])
        y = hybrid_mixer(u, w_in[l], rwkv_shift_mix[l], rwkv_w0[l], rwkv_w_up[l], rwkv_a0[l],
                         rwkv_a_up[l], rwkv_g_up[l], rwkv_k_k[l], rwkv_k_a[l], rwkv_r_k[l],
                         rwkv_gn_w[l], rwkv_gn_b[l], gdn_conv_w[l], gdn_a_log[l], gdn_dt_bias[l],
                         gdn_norm_w[l], w_branch_rwkv[l], w_branch_gdn[l], w_mix_out[l])
        h = h + rms_norm(y, mix_norm_post[l])
        c = rms_norm(h, xa_norm_pre[l])
        m = rms_norm(mem, xa_norm_mem[l])
        h = h + rms_norm(memory_cross_attention(c, m, xa_w_q[l], xa_w_kv[l], xa_w_o[l]), xa_norm_post[l])
        f = rms_norm(h, mlp_norm_pre[l])
        h = h + rms_norm(squared_relu_mlp(f, mlp_w_up[l], mlp_w_down[l]), mlp_norm_post[l])
    return h
```

```python
import numpy as np
import concourse.bass as bass
import concourse.mybir as mybir
from concourse.bass_utils import run_bass_kernel_spmd

F32 = mybir.dt.float32
BF16 = mybir.dt.bfloat16
AF = mybir.ActivationFunctionType
ALU = mybir.AluOpType
AX = mybir.AxisListType

SEM_WRAP = 30000
LAM = 0.6065306597126334


class Ev:
    __slots__ = ("key", "val")

    def __init__(self, key, val):
        self.key = key
        self.val = val


class Tl:
    def __init__(self, t, name, excl=False):
        self.t = t
        self.name = name
        self.w = None
        self.r = []
        self.excl = excl

    def __getitem__(self, idx):
        return self.t[idx]


class Deferred:
    def __init__(self, st):
        self.st = st
        self.idx = [0] * len(st)
        self.left = sum(len(l) for l in st)

    def step(self, n=1):
        while n > 0 and self.left > 0:
            for k, lst in enumerate(self.st):
                if self.idx[k] < len(lst):
                    lst[self.idx[k]]()
                    self.idx[k] += 1
                    self.left -= 1
                    n -= 1
                    if n <= 0:
                        break

    def finish(self, pools=()):
        self.step(self.left)
        for p in pools:
            p.end_streams()


class Ctx:
    _SID = 0

    def __init__(self, nc):
        self.nc = nc
        self.E = {"pe": nc.tensor, "dve": nc.vector, "act": nc.scalar, "pool": nc.gpsimd, "sp": nc.sync}
        self.cnt = {k: 0 for k in self.E}
        self.sems = {}
        self.waited = {k: {} for k in self.E}
        self.dmacnt = {}
        self.ndma = 0
        self.nsem = 0
        self.ninst = 0
        self.limit = 10 ** 9
        self.streams = None
        self.cur = -1
        self.trace = None

    def _sem(self, key):
        s = self.sems.get(key)
        if s is None:
            s = self.nc.alloc_semaphore(name="s%d" % self.nsem)
            self.nsem += 1
            self.sems[key] = s
        return s

    def _wait(self, eng, ev):
        if ev is None:
            return
        w = self.waited[eng]
        if w.get(ev.key, 0) >= ev.val:
            return
        self.E[eng].wait_ge(self._sem(ev.key), ev.val)
        w[ev.key] = ev.val

    def deps(self, eng, reads, writes):
        for r in reads:
            if r.w is not None:
                self._wait(eng, r.w)
        for r in writes:
            if r.w is not None:
                self._wait(eng, r.w)
            for e in r.r:
                self._wait(eng, e)

    def commit(self, ev, reads, writes):
        for r in writes:
            r.w = ev
            r.r = []
        for r in reads:
            if r in writes:
                continue
            r.r.append(ev)
            if len(r.r) > 10:
                d = {}
                for e in r.r:
                    if e.key not in d or d[e.key].val < e.val:
                        d[e.key] = e
                r.r = list(d.values())

    def _tick(self, eng):
        if self.trace is not None:
            import sys
            f = sys._getframe(2)
            ln = []
            while f is not None and len(ln) < 4:
                ln.append(f.f_lineno)
                f = f.f_back
            self.trace.append((self.ninst, eng, ln))
        self.cnt[eng] += 1
        c = self.cnt[eng]
        blk, val = divmod(c, SEM_WRAP)
        if val == 0:
            blk -= 1
            val = SEM_WRAP
        return Ev((eng, blk), val)

    def begin_stream(self):
        if self.streams is None:
            self.streams = []
            self.sids = []
        self.streams.append([])
        Ctx._SID += 1
        self.sids.append(Ctx._SID)
        self.cur = len(self.streams) - 1

    @property
    def sid(self):
        return self.sids[self.cur]

    def take_streams(self):
        st, self.streams, self.cur = self.streams, None, -1
        return Deferred(st)

    def flush_streams(self, pools=()):
        st, self.streams, self.cur = self.streams, None, -1
        idx = [0] * len(st)
        live = True
        while live:
            live = False
            for k, lst in enumerate(st):
                if idx[k] < len(lst):
                    lst[idx[k]]()
                    idx[k] += 1
                    live = True
        for p in pools:
            p.end_streams()

    def op(self, eng, fn, reads=(), writes=()):
        if self.streams is not None:
            reads, writes = list(reads), list(writes)
            self.streams[self.cur].append(lambda: self._op(eng, fn, reads, writes))
            return None
        return self._op(eng, fn, reads, writes)

    def _op(self, eng, fn, reads=(), writes=()):
        if self.ninst >= self.limit:
            return None
        if any(r.excl for r in reads):
            writes = list(writes) + [r for r in reads if r.excl and r not in writes]
        self.deps(eng, reads, writes)
        ins = fn()
        ev = self._tick(eng)
        ins.then_inc(self._sem(ev.key), 1)
        self.commit(ev, reads, writes)
        self.ninst += 1
        return ev

    def pe(self, fns, reads=(), writes=()):
        if self.streams is not None:
            fns, reads, writes = list(fns), list(reads), list(writes)
            self.streams[self.cur].append(lambda: self._pe(fns, reads, writes))
            return None
        return self._pe(fns, reads, writes)

    def pe_steps(self, steps, writes=()):
        if self.streams is not None:
            steps, writes = list(steps), list(writes)
            self.streams[self.cur].append(lambda: self._pe_steps(steps, writes))
            return None
        return self._pe_steps(steps, writes)

    def _pe_steps(self, steps, writes=()):
        if self.ninst >= self.limit:
            return None
        self.deps("pe", [], writes)
        ins = None
        allr = []
        for fn, reads in steps:
            self.deps("pe", reads, [])
            ins = fn()
            for r in reads:
                if r not in allr:
                    allr.append(r)
        ev = self._tick("pe")
        ins.then_inc(self._sem(ev.key), 1)
        self.commit(ev, allr, writes)
        self.ninst += len(steps)
        return ev

    def _pe(self, fns, reads=(), writes=()):
        if self.ninst >= self.limit:
            return None
        if any(r.excl for r in reads):
            writes = list(writes) + [r for r in reads if r.excl and r not in writes]
        self.deps("pe", reads, writes)
        ins = None
        for fn in fns:
            ins = fn()
        ev = self._tick("pe")
        ins.then_inc(self._sem(ev.key), 1)
        self.commit(ev, reads, writes)
        self.ninst += len(fns)
        return ev

    def dma(self, q, out, in_, reads=(), writes=(), **kw):
        if self.streams is not None:
            reads, writes = list(reads), list(writes)
            self.streams[self.cur].append(lambda: self._dma(q, out, in_, reads, writes, **kw))
            return None
        return self._dma(q, out, in_, reads, writes, **kw)

    def _dma(self, q, out, in_, reads=(), writes=(), **kw):
        if self.ninst >= self.limit:
            return None
        self.deps(q, reads, writes)
        semkey = ("dma", self.ndma % 32)
        self.ndma += 1
        prev = self.dmacnt.get(semkey, 0)
        if prev:
            self._wait(q, Ev(semkey, prev))
        c = prev + 16
        self.dmacnt[semkey] = c
        ins = self.E[q].dma_start(out=out, in_=in_, **kw)
        ins.then_inc(self._sem(semkey), 16)
        ev = Ev(semkey, c)
        self.commit(ev, reads, writes)
        self.ninst += 1
        return ev

    def wait_all(self, eng, regs):
        for r in regs:
            if r.w is not None:
                self._wait(eng, r.w)
            for e in r.r:
                self._wait(eng, e)

    def barrier(self):
        evs = []
        for eng, c in self.cnt.items():
            if c:
                blk, val = divmod(c, SEM_WRAP)
                if val == 0:
                    blk -= 1
                    val = SEM_WRAP
                evs.append(Ev((eng, blk), val))
        for k, c in self.dmacnt.items():
            evs.append(Ev(k, c))
        for eng in self.E:
            for ev in evs:
                self._wait(eng, ev)


_UID = [0]


def sb(nc, name, shape, dtype):
    _UID[0] += 1
    return nc.alloc_sbuf_tensor("%s_u%d" % (name, _UID[0]), shape, dtype)


def pb_(nc, name, shape, dtype):
    _UID[0] += 1
    return nc.alloc_psum_tensor("%s_u%d" % (name, _UID[0]), shape, dtype)


class Pool:
    def __init__(self, nc, name, n, shape, dtype, psum=False, ctx=None):
        self.ctx = ctx
        self.sfree = {}
        self.owned = {}
        self.quota = None
        self.free = []
        for i in range(n):
            if psum:
                t = pb_(nc, "%s%d" % (name, i), shape, dtype)
            else:
                t = sb(nc, "%s%d" % (name, i), shape, dtype)
            self.free.append(Tl(t, "%s%d" % (name, i), excl=psum))

    def get(self):
        c = self.ctx
        if c is not None and c.streams is not None:
            l = self.sfree.get(c.sid)
            n = self.owned.get(c.sid, 0)
            if l and not (self.quota is not None and n < self.quota and self.free):
                return l.pop(0)
            self.owned[c.sid] = n + 1
        return self.free.pop(0)

    def put(self, *ts):
        c = self.ctx
        if c is not None and c.streams is not None:
            self.sfree.setdefault(c.sid, []).extend(ts)
            return
        for t in ts:
            self.free.append(t)

    def end_streams(self):
        for l in self.sfree.values():
            self.free.extend(l)
        self.sfree = {}
        self.owned = {}


class Ops:
    def __init__(self, C):
        self.C = C
        self.nc = C.nc

    def ts(self, out, in0, s1, s2, op0, op1=None, R=(), W=(), eng="dve"):
        e = self.nc.vector if eng == "dve" else self.nc.gpsimd
        if op1 is None:
            return self.C.op(eng, lambda: e.tensor_scalar(out=out, in0=in0, scalar1=s1, scalar2=None, op0=op0), R, W)
        return self.C.op(eng, lambda: e.tensor_scalar(out=out, in0=in0, scalar1=s1, scalar2=s2, op0=op0, op1=op1), R, W)

    def stt(self, out, in0, s, in1, op0, op1, R=(), W=()):
        return self.C.op("dve", lambda: self.nc.vector.scalar_tensor_tensor(out=out, in0=in0, scalar=s, in1=in1, op0=op0, op1=op1), R, W)

    def tt(self, out, in0, in1, op, R=(), W=(), eng="dve"):
        e = self.nc.vector if eng == "dve" else self.nc.gpsimd
        return self.C.op(eng, lambda: e.tensor_tensor(out=out, in0=in0, in1=in1, op=op), R, W)

    def act(self, out, in_, func, R=(), W=(), bias=None, scale=None, accum=None):
        kw = {}
        if bias is not None:
            kw["bias"] = bias
        if scale is not None:
            kw["scale"] = scale
        if accum is not None:
            kw["accum_out"] = accum
        return self.C.op("act", lambda: self.nc.scalar.activation(out=out, in_=in_, func=func, **kw), R, W)

    def copy(self, out, in_, R=(), W=(), eng="dve"):
        if eng == "act":
            return self.C.op("act", lambda: self.nc.scalar.copy(out=out, in_=in_), R, W)
        e = self.nc.vector if eng == "dve" else self.nc.gpsimd
        return self.C.op(eng, lambda: e.tensor_copy(out=out, in_=in_), R, W)

    def recip(self, out, in_, R=(), W=()):
        return self.C.op("dve", lambda: self.nc.vector.reciprocal(out=out, in_=in_), R, W)

    def reduce(self, out, in_, op, R=(), W=()):
        return self.C.op("dve", lambda: self.nc.vector.tensor_reduce(out=out, in_=in_, axis=AX.X, op=op), R, W)

    def scan(self, out, d0, d1, R=(), W=()):
        return self.C.op("dve", lambda: self.nc.vector.tensor_tensor_scan(out=out, data0=d0, data1=d1, initial=0.0, op0=ALU.mult, op1=ALU.add), R, W)

    def memset(self, ap, val, W=(), eng="dve"):
        e = self.nc.vector if eng == "dve" else self.nc.gpsimd
        return self.C.op(eng, lambda: e.memset(ap, val), (), W)

    def mm(self, out, lhsT, rhs, start=True, stop=True):
        return lambda: self.nc.tensor.matmul(out, lhsT, rhs, start=start, stop=stop)

    def tr(self, out, in_, ident):
        return lambda: self.nc.tensor.transpose(out, in_, ident)


def _layout(items):
    off = {}
    n = 0
    for name, w in items:
        off[name] = n
        n += w
    return off, n


CST_OFF, CST_N = _layout([
    ("ident", 128), ("ones", 128), ("bones", 128), ("MU", 512), ("MUI", 512), ("I8", 512), ("HI", 2),
    ("SelB", 512), ("SelG", 512), ("eps6", 1), ("epsgn", 1), ("one", 1), ("zero", 1),
])


def make_consts():
    c = np.zeros((128, CST_N), np.float32)
    o = CST_OFF
    c[:, o["ident"]:o["ident"] + 128] = np.eye(128, dtype=np.float32)
    c[:, o["ones"]:o["ones"] + 128] = 1.0
    c[0:64, o["bones"]:o["bones"] + 64] = 1.0
    c[64:128, o["bones"] + 64:o["bones"] + 128] = 1.0
    s = np.arange(64)[:, None]
    t = np.arange(64)[None, :]
    for h in range(8):
        c[0:64, o["MU"] + h * 64:o["MU"] + (h + 1) * 64] = (t > s)
        c[0:64, o["MUI"] + h * 64:o["MUI"] + (h + 1) * 64] = (t >= s)
        c[0:64, o["I8"] + h * 64:o["I8"] + (h + 1) * 64] = (t == s)
    c[0:64, o["HI"]] = 1.0
    c[64:128, o["HI"] + 1] = 1.0
    for h in range(4):
        c[h, o["SelB"] + h * 128:o["SelB"] + (h + 1) * 128] = 1.0
        c[32 + h, o["SelG"] + h * 128:o["SelG"] + (h + 1) * 128] = 1.0
    c[:, o["eps6"]] = 1e-6
    c[:, o["epsgn"]] = 64e-5
    c[:, o["one"]] = 1.0
    return c


_prm_items = []
for _n in ("mix_r", "mix_k", "mix_v"):
    _prm_items += [("%s%d" % (_n, c), 1) for c in range(4)]
_prm_items += [("mix_lw", 1), ("mix_la", 1), ("mix_lg0", 1), ("mix_lg1", 1)]
for _n in ("w0_", "a0_", "kk_", "ka_", "rk_"):
    _prm_items += [("%s%d" % (_n, c), 1) for c in range(4)]
_prm_items += [("cw%d_%d" % (j, i), 1) for j in range(4) for i in range(12)]
_prm_items += [("dtb", 1), ("alog", 1)]
PRM_OFF, PRM_N = _layout(_prm_items)
MIX_NAMES = ["mix_r%d" % c for c in range(4)] + ["mix_k%d" % c for c in range(4)] + ["mix_v%d" % c for c in range(4)] + [
    "mix_lw", "mix_la", "mix_lg0", "mix_lg1"]
DRV_OFF, DRV_N = _layout([("om_" + n, 1) for n in MIX_NAMES] + [("omka_%d" % c, 1) for c in range(4)] + [("nea", 1)])

NCHA = 33


def pack_mixer_params(inp, g):
    P = np.zeros((128, PRM_N), np.float32)
    sm = inp["rwkv_shift_mix"][0]
    ch0 = g * 512
    for c in range(4):
        sl = slice(ch0 + c * 128, ch0 + (c + 1) * 128)
        P[:, PRM_OFF["mix_r%d" % c]] = sm[0 * 2048:][sl]
        P[:, PRM_OFF["mix_k%d" % c]] = sm[1 * 2048:][sl]
        P[:, PRM_OFF["mix_v%d" % c]] = sm[2 * 2048:][sl]
        P[:, PRM_OFF["w0_%d" % c]] = inp["rwkv_w0"][0][sl]
        P[:, PRM_OFF["a0_%d" % c]] = inp["rwkv_a0"][0][sl]
        P[:, PRM_OFF["kk_%d" % c]] = inp["rwkv_k_k"][0][sl]
        P[:, PRM_OFF["ka_%d" % c]] = inp["rwkv_k_a"][0][sl]
        P[:, PRM_OFF["rk_%d" % c]] = inp["rwkv_r_k"][0].reshape(-1)[sl]
    P[0:96, PRM_OFF["mix_lw"]] = sm[6144:6240]
    P[0:96, PRM_OFF["mix_la"]] = sm[6240:6336]
    P[:, PRM_OFF["mix_lg0"]] = sm[6336:6464]
    P[:, PRM_OFF["mix_lg1"]] = sm[6464:6592]
    cw = inp["gdn_conv_w"][0]
    for j in range(4):
        for part in range(3):
            for h in range(4):
                col = part * 2048 + (4 * g + h) * 128
                P[:, PRM_OFF["cw%d_%d" % (j, part * 4 + h)]] = cw[j, col:col + 128]
    P[32:36, PRM_OFF["dtb"]] = inp["gdn_dt_bias"][0][4 * g:4 * g + 4]
    P[32:36, PRM_OFF["alog"]] = inp["gdn_a_log"][0][4 * g:4 * g + 4]
    bc = np.zeros((64, 3, 512), np.float32)
    bc[:, 0, :] = inp["rwkv_gn_w"][0][ch0:ch0 + 512][None, :]
    bc[:, 1, :] = inp["rwkv_gn_b"][0][ch0:ch0 + 512][None, :]
    bc[:, 2, :] = np.tile(inp["gdn_norm_w"][0], 4)[None, :]
    lora = np.zeros((128, 4, 512), np.float32)
    lora[0:96, 0, :] = inp["rwkv_w_up"][0][:, ch0:ch0 + 512]
    lora[0:96, 1, :] = inp["rwkv_a_up"][0][:, ch0:ch0 + 512]
    lora[:, 2, :] = inp["rwkv_g_up"][0][0:128, ch0:ch0 + 512]
    lora[:, 3, :] = inp["rwkv_g_up"][0][128:256, ch0:ch0 + 512]
    return P, bc, lora


def mixer_col_index(g):
    idx = -np.ones(NCHA * 128, np.int64)
    RW = 2048
    for part in range(3):
        for c in range(4):
            j = part * 4 + c
            idx[j * 128:(j + 1) * 128] = part * RW + g * 512 + c * 128 + np.arange(128)
    idx[12 * 128:12 * 128 + 96] = 3 * RW + np.arange(96)
    idx[13 * 128:13 * 128 + 96] = 3 * RW + 96 + np.arange(96)
    idx[14 * 128:16 * 128] = 3 * RW + 192 + np.arange(256)
    base = 6592
    for part in range(4):
        for h in range(4):
            j = 16 + part * 4 + h
            idx[j * 128:(j + 1) * 128] = base + part * 2048 + (4 * g + h) * 128 + np.arange(128)
    for h in range(4):
        idx[32 * 128 + h] = base + 8192 + 4 * g + h
        idx[32 * 128 + 32 + h] = base + 8192 + 16 + 4 * g + h
    return idx


TP = 256
YBLK = 512
CH = 64
NCK = TP // CH


class Prog:
    def __init__(self, nc):
        self.nc = nc
        self.C = Ctx(nc)
        self.O = Ops(self.C)
        self.PS = Pool(nc, "ps", 8, [128, 512], F32, psum=True, ctx=self.C)

    def load_consts(self, cst_d):
        nc, C = self.nc, self.C
        self.CST = Tl(sb(nc, "sb_cst", [128, CST_N], F32), "cst")
        C.dma("sp", self.CST[:, :], cst_d, writes=[self.CST])

    def cst(self, name, w, rows=slice(0, 128)):
        o = CST_OFF[name]
        return self.CST[rows, o:o + w]

    def phase_A2(self, NT, pT, pT_regs, prm_d, bc_d, lora_d, y_blocks, y_regs, after_block=None, ydt=BF16, after_pass=None):
        nc, C, O, PS = self.nc, self.C, self.O, self.PS
        mark = (nc.sbuf_base, nc.sbuf_top)
        FM = Pool(nc, "fm", 102, [128, TP + 4], F32, ctx=C)
        CHP = Pool(nc, "chp", 24, [64, 512], F32)
        CH16 = Pool(nc, "ch16", 14, [64, 512], BF16)
        CHB = Pool(nc, "chb", 4, [64, 512], ydt)
        SM = Pool(nc, "sm", 8, [128, 64], F32)
        PRM = Tl(sb(nc, "sb_prm", [128, PRM_N], F32), "prm")
        DRV = Tl(sb(nc, "drv", [128, DRV_N], F32), "drv")
        BCP = Tl(sb(nc, "sb_bcp", [64, 3, 512], F32), "bcp")
        LORA = Tl(sb(nc, "sb_lora", [128, 4, 512], F32), "lora")
        BG = Tl(sb(nc, "bg", [128, TP], F32), "bg")
        GCR = Tl(sb(nc, "gcr", [128, TP], F32), "gcr")
        E1 = Tl(sb(nc, "e1", [128, TP], F32), "e1")
        ST = [Tl(sb(nc, "strw%d" % i, [128, 64], F32), "strw%d" % i) for i in range(8)]
        SG_ = [Tl(sb(nc, "stgd%d" % i, [128, 128], F32), "stgd%d" % i) for i in range(4)]
        C.dma("sp", PRM[:, :], prm_d, writes=[PRM])
        C.dma("sp", BCP[:, :, :], bc_d, writes=[BCP])
        C.dma("sp", LORA[:, :, :], lora_d, writes=[LORA])
        for t in ST + SG_:
            O.memset(t[:, :], 0.0, W=[t])
        O.memset(BG[:, :], 0.0, W=[BG])
        O.memset(GCR[:, :], 0.0, W=[GCR])
        O.memset(E1[:, :], 0.0, W=[E1])
        for n in MIX_NAMES:
            a, b = PRM_OFF[n], DRV_OFF["om_" + n]
            O.ts(DRV[:, b:b + 1], PRM[:, a:a + 1], -1.0, 1.0, ALU.mult, ALU.add, R=[PRM], W=[DRV])
        for c in range(4):
            a, b = PRM_OFF["ka_%d" % c], DRV_OFF["omka_%d" % c]
            O.ts(DRV[:, b:b + 1], PRM[:, a:a + 1], -1.0, 1.0, ALU.mult, ALU.add, R=[PRM], W=[DRV])
        a, b = PRM_OFF["alog"], DRV_OFF["nea"]
        O.act(DRV[:, b:b + 1], PRM[:, a:a + 1], AF.Exp, R=[PRM, DRV], W=[DRV])
        O.ts(DRV[:, b:b + 1], DRV[:, b:b + 1], -1.0, None, ALU.mult, R=[DRV], W=[DRV])
        C.barrier()

        def pc(name, rows=slice(0, 128)):
            o = PRM_OFF[name]
            return PRM[rows, o:o + 1]

        def dc(name, rows=slice(0, 128)):
            o = DRV_OFF[name]
            return DRV[rows, o:o + 1]

        IDENT = self.cst("ident", 128)
        ONES = self.cst("ones", 128)
        BONES = self.cst("bones", 128)
        EPS6 = self.cst("eps6", 1)
        EPSGN = self.cst("epsgn", 1)
        W_ = slice(0, TP)
        R64 = slice(0, 64)

        def mul(a, b, rows=slice(0, 128), eng="dve"):
            o = FM.get()
            O.tt(o[rows, W_], a[rows, W_], b[rows, W_], ALU.mult, R=[a, b], W=[o], eng=eng)
            return o

        def inverse(Q0, nh):
            Wd = nh * 64
            hcs = [slice(h * 64, (h + 1) * 64) for h in range(nh)]
            ps = PS.get()
            C.pe([O.tr(ps[R64, hc], Q0[R64, hc], IDENT[R64, 0:64]) for hc in hcs], [Q0], [ps])
            QT = CH16.get()
            O.copy(QT[R64, 0:Wd], ps[R64, 0:Wd], R=[ps], W=[QT], eng="act")
            PS.put(ps)
            Q = CH16.get()
            O.copy(Q[R64, 0:Wd], Q0[R64, 0:Wd], R=[Q0], W=[Q], eng="pool")
            XT = CH16.get()
            O.tt(XT[R64, 0:Wd], Q0[R64, 0:Wd], self.cst("I8", Wd, R64), ALU.add, R=[Q0], W=[XT])
            CHP.put(Q0)
            yield
            for j in range(1, 6):
                ps1 = PS.get()
                C.pe([O.mm(ps1[R64, hc], Q[R64, hc], QT[R64, hc]) for hc in hcs], [Q, QT], [ps1])
                QTn = CH16.get()
                O.copy(QTn[R64, 0:Wd], ps1[R64, 0:Wd], R=[ps1], W=[QTn], eng="act")
                PS.put(ps1)
                Qn = None
                if j < 5:
                    ps2 = PS.get()
                    C.pe([O.mm(ps2[R64, hc], QT[R64, hc], Q[R64, hc]) for hc in hcs], [Q, QT], [ps2])
                    Qn = CH16.get()
                    O.copy(Qn[R64, 0:Wd], ps2[R64, 0:Wd], R=[ps2], W=[Qn], eng="dve")
                    PS.put(ps2)
                yield
                ps3 = PS.get()
                C.pe([O.mm(ps3[R64, hc], QTn[R64, hc], XT[R64, hc]) for hc in hcs], [QTn, XT], [ps3])
                XTn = CH16.get() if j < 5 else CHP.get()
                O.tt(XTn[R64, 0:Wd], XT[R64, 0:Wd], ps3[R64, 0:Wd], ALU.add, R=[XT, ps3], W=[XTn])
                PS.put(ps3)
                CH16.put(Q, QT, XT)
                Q, QT, XT = Qn, QTn, XTn
                yield
            CH16.put(QT)
            return XT

        MU = self.cst("MU", 512, R64)
        MUI = self.cst("MUI", 512, R64)

        for s in range(NT // TP):
            t0 = s * TP

            def load_p(j, q="sp"):
                t = FM.get()
                C.dma(q, t[:, 0:TP + 4], pT[j, :, t0:t0 + TP + 4], reads=[pT_regs[j]], writes=[t])
                return t

            def xmix(P, name, rows=slice(0, 128)):
                t1 = FM.get()
                xm = FM.get()
                O.ts(t1[rows, W_], P[rows, 3:3 + TP], pc(name, rows), None, ALU.mult, R=[P], W=[t1], eng="pool")
                O.stt(xm[rows, W_], P[rows, 4:4 + TP], dc("om_" + name, rows), t1[rows, W_], ALU.mult, ALU.add, R=[P, t1], W=[xm])
                FM.put(t1, P)
                return xm

            R96 = slice(0, 96)
            XLW = xmix(load_p(12), "mix_lw", R96)
            O.act(XLW[R96, W_], XLW[R96, W_], AF.Tanh, R=[XLW], W=[XLW])
            XLA = xmix(load_p(13), "mix_la", R96)
            SLG = []
            for kc in range(2):
                x = xmix(load_p(14 + kc), "mix_lg%d" % kc)
                O.act(x[:, W_], x[:, W_], AF.Sigmoid, R=[x], W=[x])
                SLG.append(x)
            XV, RK, EW, Rt, At, Kt, Bt, Kh, Bh = ([None] * 4 for _ in range(9))
            for c in range(4):
                C.begin_stream()
                csl = slice(c * 128, (c + 1) * 128)
                XR = xmix(load_p(0 + c), "mix_r%d" % c)
                XK = xmix(load_p(4 + c), "mix_k%d" % c)
                XV[c] = xmix(load_p(8 + c), "mix_v%d" % c)
                ps = PS.get()
                C.pe([O.mm(ps[:, W_], LORA[R96, 0, csl], XLW[R96, W_])], [XLW], [ps])
                SG = FM.get()
                O.act(SG[:, W_], ps[:, W_], AF.Sigmoid, R=[ps], W=[SG], bias=pc("w0_%d" % c))
                PS.put(ps)
                ps = PS.get()
                C.pe([O.mm(ps[:, W_], LORA[R96, 1, csl], XLA[R96, W_])], [XLA], [ps])
                AA = FM.get()
                O.act(AA[:, W_], ps[:, W_], AF.Sigmoid, R=[ps], W=[AA], bias=pc("a0_%d" % c))
                PS.put(ps)
                KX = FM.get()
                O.ts(KX[:, W_], XK[:, W_], pc("kk_%d" % c), None, ALU.mult, R=[XK], W=[KX], eng="pool")
                SQ = FM.get()
                O.act(SQ[:, W_], KX[:, W_], AF.Square, R=[KX], W=[SQ])
                ps = PS.get()
                C.pe([O.mm(ps[:, W_], BONES, SQ[:, W_])], [SQ], [ps])
                RN = FM.get()
                O.act(RN[:, W_], ps[:, W_], AF.Ln, R=[ps], W=[RN], bias=EPS6)
                PS.put(ps)
                FM.put(SQ)
                O.act(RN[:, W_], RN[:, W_], AF.Exp, R=[RN], W=[RN], scale=-0.5)
                KK = mul(KX, RN, eng="pool")
                FM.put(KX, RN)
                T1 = FM.get()
                O.ts(T1[:, W_], AA[:, W_], pc("ka_%d" % c), dc("omka_%d" % c), ALU.mult, ALU.add, R=[AA], W=[T1])
                KP = mul(XK, T1, eng="pool")
                Bv = mul(KK, AA, eng="pool")
                FM.put(XK, T1, AA)
                rk = FM.get()
                O.stt(rk[:, W_], XR[:, W_], pc("rk_%d" % c), KP[:, W_], ALU.mult, ALU.mult, R=[XR, KP], W=[rk])
                RK[c] = rk
                CS = FM.get()
                for q in range(NCK):
                    qs = slice(q * CH, (q + 1) * CH)
                    O.scan(CS[:, qs], ONES[:, 0:CH], SG[:, qs], R=[SG], W=[CS])
                ew = FM.get()
                O.act(ew[:, W_], CS[:, W_], AF.Exp, R=[CS], W=[ew], scale=-LAM)
                EW[c] = ew
                EWI = FM.get()
                O.act(EWI[:, W_], CS[:, W_], AF.Exp, R=[CS], W=[EWI], scale=LAM)
                CSX = FM.get()
                O.tt(CSX[:, W_], CS[:, W_], SG[:, W_], ALU.subtract, R=[CS, SG], W=[CSX], eng="pool")
                O.act(CSX[:, W_], CSX[:, W_], AF.Exp, R=[CSX], W=[CSX], scale=-LAM)
                FM.put(SG)
                DF = FM.get()
                for q in range(NCK):
                    qs = slice(q * CH, (q + 1) * CH)
                    e = (q + 1) * CH - 1
                    O.ts(DF[:, qs], CS[:, qs], CS[:, e:e + 1], None, ALU.subtract, R=[CS], W=[DF])
                O.act(DF[:, W_], DF[:, W_], AF.Exp, R=[DF], W=[DF], scale=LAM)
                FM.put(CS)
                Rt[c] = mul(XR, ew)
                FM.put(XR)
                a_ = FM.get()
                O.stt(a_[:, W_], KK[:, W_], -1.0, CSX[:, W_], ALU.mult, ALU.mult, R=[KK, CSX], W=[a_])
                At[c] = a_
                FM.put(KK, CSX)
                km, bm = [], []
                for hh in range(2):
                    hi = self.CST[:, CST_OFF["HI"] + hh:CST_OFF["HI"] + hh + 1]
                    k_ = FM.get()
                    O.stt(k_[:, W_], KP[:, W_], hi, EWI[:, W_], ALU.mult, ALU.mult, R=[KP, EWI], W=[k_])
                    b_ = FM.get()
                    O.stt(b_[:, W_], Bv[:, W_], hi, EWI[:, W_], ALU.mult, ALU.mult, R=[Bv, EWI], W=[b_])
                    km.append(k_)
                    bm.append(b_)
                Kt[c] = km
                Bt[c] = bm
                Kh[c] = mul(KP, DF, eng="pool")
                Bh[c] = mul(Bv, DF, eng="pool")
                FM.put(KP, Bv, EWI, DF)
            C.flush_streams([FM, PS])
            FM.put(XLW, XLA)

            def conv(j, widx):
                P = load_p(j)
                acc = FM.get()
                O.ts(acc[:, W_], P[:, 1:1 + TP], pc("cw0_%d" % widx), None, ALU.mult, R=[P], W=[acc])
                for k in range(1, 4):
                    O.stt(acc[:, W_], P[:, 1 + k:1 + k + TP], pc("cw%d_%d" % (k, widx)), acc[:, W_], ALU.mult, ALU.add, R=[P, acc], W=[acc])
                O.act(acc[:, W_], acc[:, W_], AF.Silu, R=[acc], W=[acc])
                FM.put(P)
                return acc

            def l2n(X, scale=None):
                SQ = FM.get()
                O.act(SQ[:, W_], X[:, W_], AF.Square, R=[X], W=[SQ])
                ps = PS.get()
                C.pe([O.mm(ps[:, W_], ONES, SQ[:, W_])], [SQ], [ps])
                RN = FM.get()
                O.act(RN[:, W_], ps[:, W_], AF.Ln, R=[ps], W=[RN], bias=EPS6)
                PS.put(ps)
                O.act(RN[:, W_], RN[:, W_], AF.Exp, R=[RN], W=[RN], scale=-0.5)
                o = FM.get()
                if scale is None:
                    O.tt(o[:, W_], X[:, W_], RN[:, W_], ALU.mult, R=[X, RN], W=[o])
                else:
                    O.stt(o[:, W_], X[:, W_], scale, RN[:, W_], ALU.mult, ALU.mult, R=[X, RN], W=[o])
                FM.put(SQ, RN, X)
                return o

            Pba = load_p(32)
            R4 = slice(0, 4)
            RA = slice(32, 36)
            O.act(BG[R4, W_], Pba[R4, 4:4 + TP], AF.Sigmoid, R=[Pba, BG], W=[BG])
            O.act(E1[RA, W_], Pba[RA, 4:4 + TP], AF.Exp, R=[Pba, E1], W=[E1], bias=pc("dtb", RA))
            O.act(E1[RA, W_], E1[RA, W_], AF.Ln, R=[E1], W=[E1], bias=self.cst("one", 1, RA))
            O.ts(BG[RA, W_], E1[RA, W_], dc("nea", RA), None, ALU.mult, R=[E1, BG], W=[BG])
            for q in range(NCK):
                qs = slice(q * CH, (q + 1) * CH)
                O.scan(GCR[RA, qs], ONES[RA, 0:CH], BG[RA, qs], R=[BG, GCR], W=[GCR])
            FM.put(Pba)
            QN, KN, KB, KBG, VB, QD, KD, SZ, GCB, EGC = [], [], [], [], [], [], [], [], [], []
            for h in range(4):
                C.begin_stream()
                qn = l2n(conv(16 + h, 0 + h), scale=float(128 ** -0.5))
                kn = l2n(conv(20 + h, 4 + h))
                cv = conv(24 + h, 8 + h)
                Pz = load_p(28 + h)
                sz = FM.get()
                O.act(sz[:, W_], Pz[:, 4:4 + TP], AF.Silu, R=[Pz], W=[sz])
                FM.put(Pz)
                ps = PS.get()
                C.pe([O.mm(ps[:, W_], self.CST[R64, CST_OFF["SelB"] + h * 128:CST_OFF["SelB"] + (h + 1) * 128], BG[R64, W_])], [BG], [ps])
                beta = FM.get()
                O.copy(beta[:, W_], ps[:, W_], R=[ps], W=[beta], eng="act")
                PS.put(ps)
                ps = PS.get()
                C.pe([O.mm(ps[:, W_], self.CST[R64, CST_OFF["SelG"] + h * 128:CST_OFF["SelG"] + (h + 1) * 128], GCR[R64, W_])], [GCR], [ps])
                gcb = FM.get()
                O.copy(gcb[:, W_], ps[:, W_], R=[ps], W=[gcb], eng="dve")
                egc = FM.get()
                O.act(egc[:, W_], ps[:, W_], AF.Exp, R=[ps], W=[egc])
                PS.put(ps)
                kb = mul(kn, beta, eng="pool")
                kbg = mul(kb, egc, eng="pool")
                vb = mul(cv, beta, eng="pool")
                qd = mul(qn, egc)
                DF = FM.get()
                for q in range(NCK):
                    qs = slice(q * CH, (q + 1) * CH)
                    e = (q + 1) * CH - 1
                    O.ts(DF[:, qs], gcb[:, qs], gcb[:, e:e + 1], -1.0, ALU.subtract, ALU.mult, R=[gcb], W=[DF])
                O.act(DF[:, W_], DF[:, W_], AF.Exp, R=[DF], W=[DF])
                kd = mul(kn, DF, eng="pool")
                FM.put(cv, beta, DF)
                QN.append(qn); KN.append(kn); KB.append(kb); KBG.append(kbg); VB.append(vb)
                QD.append(qd); KD.append(kd); SZ.append(sz); GCB.append(gcb); EGC.append(egc)
                if h == 3:
                    C.flush_streams([FM, PS])

            bg = []

            def rw_chunk(q):
                qs = slice(q * CH, (q + 1) * CH)
                qe = (q + 1) * CH - 1
                tok = t0 + q * CH
                psV, psB, psK = PS.get(), PS.get(), PS.get()
                C.pe([O.tr(psV[R64, c * 128:(c + 1) * 128], XV[c][:, qs], IDENT) for c in range(4)], XV, [psV])
                C.pe([O.tr(psB[R64, c * 128:(c + 1) * 128], Bh[c][:, qs], IDENT) for c in range(4)], Bh, [psB])
                C.pe([O.tr(psK[R64, c * 128:(c + 1) * 128], Kh[c][:, qs], IDENT) for c in range(4)], Kh, [psK])
                Vt, Bht, Kht = CHP.get(), CHP.get(), CHP.get()
                O.copy(Vt[R64, :], psV[R64, :], R=[psV], W=[Vt], eng="act")
                O.copy(Bht[R64, :], psB[R64, :], R=[psB], W=[Bht], eng="dve")
                O.copy(Kht[R64, :], psK[R64, :], R=[psK], W=[Kht], eng="act")
                PS.put(psV, psB, psK)
                yield
                psN, psAK, psRB, psRK = PS.get(), PS.get(), PS.get(), PS.get()
                fN, fAK, fRB, fRK = [], [], [], []
                for h in range(8):
                    c, hh = h // 2, h % 2
                    PR = slice(hh * 64, hh * 64 + 64)
                    hc = slice(h * 64, (h + 1) * 64)
                    fN.append(O.mm(psN[R64, hc], Bt[c][hh][:, qs], At[c][:, qs]))
                    fAK.append(O.mm(psAK[R64, hc], Kt[c][hh][:, qs], At[c][:, qs]))
                    fRB.append(O.mm(psRB[R64, hc], Bt[c][hh][:, qs], Rt[c][:, qs]))
                    fRK.append(O.mm(psRK[R64, hc], Kt[c][hh][:, qs], Rt[c][:, qs]))
                Btf = [t for p_ in Bt for t in p_]
                Ktf = [t for p_ in Kt for t in p_]
                C.pe(fN, Btf + At, [psN])
                C.pe(fAK, Ktf + At, [psAK])
                C.pe(fRB, Btf + Rt, [psRB])
                C.pe(fRK, Ktf + Rt, [psRK])
                Q0, AKm, RBm, RKm = CHP.get(), CHP.get(), CHP.get(), CHP.get()
                O.tt(Q0[R64, :], psN[R64, :], MU, ALU.mult, R=[psN], W=[Q0])
                O.tt(AKm[R64, :], psAK[R64, :], MU, ALU.mult, R=[psAK], W=[AKm])
                O.tt(RBm[R64, :], psRB[R64, :], MUI, ALU.mult, R=[psRB], W=[RBm])
                O.tt(RKm[R64, :], psRK[R64, :], MUI, ALU.mult, R=[psRK], W=[RKm])
                PS.put(psN, psAK, psRB, psRK)
                yield
                XT = yield from inverse(Q0, 8)
                psR1 = PS.get()
                f = []
                for h in range(8):
                    c, hh = h // 2, h % 2
                    PR = slice(hh * 64, hh * 64 + 64)
                    hc = slice(h * 64, (h + 1) * 64)
                    f.append(O.mm(psR1[R64, hc], At[c][:, qs], ST[h][:, :], start=True, stop=False))
                    f.append(O.mm(psR1[R64, hc], AKm[R64, hc], Vt[R64, hc], start=False, stop=True))
                C.pe(f, At + ST + [AKm, Vt], [psR1])
                R1 = CHP.get()
                O.copy(R1[R64, :], psR1[R64, :], R=[psR1], W=[R1], eng="act")
                PS.put(psR1)
                yield
                psU = PS.get()
                C.pe([O.mm(psU[R64, slice(h * 64, (h + 1) * 64)], XT[R64, slice(h * 64, (h + 1) * 64)], R1[R64, slice(h * 64, (h + 1) * 64)]) for h in range(8)], [XT, R1], [psU])
                Ut = CHP.get()
                O.copy(Ut[R64, :], psU[R64, :], R=[psU], W=[Ut], eng="dve")
                PS.put(psU)
                yield
                psY = PS.get()
                f = []
                for h in range(8):
                    c, hh = h // 2, h % 2
                    PR = slice(hh * 64, hh * 64 + 64)
                    hc = slice(h * 64, (h + 1) * 64)
                    f.append(O.mm(psY[R64, hc], Rt[c][:, qs], ST[h][:, :], start=True, stop=False))
                    f.append(O.mm(psY[R64, hc], RBm[R64, hc], Ut[R64, hc], start=False, stop=False))
                    f.append(O.mm(psY[R64, hc], RKm[R64, hc], Vt[R64, hc], start=False, stop=True))
                C.pe(f, Rt + ST + [RBm, RKm, Ut, Vt], [psY])
                psS = PS.get()
                f = []
                for c in range(4):
                    cs_ = slice(c * 128, (c + 1) * 128)
                    f.append(O.mm(psS[:, cs_], Bht[R64, cs_], Ut[R64, cs_], start=True, stop=False))
                    f.append(O.mm(psS[:, cs_], Kht[R64, cs_], Vt[R64, cs_], start=False, stop=True))
                C.pe(f, [Bht, Kht, Ut, Vt], [psS])
                for h in range(8):
                    c, hh = h // 2, h % 2
                    PR = slice(hh * 64, hh * 64 + 64)
                    O.stt(ST[h][PR, :], ST[h][PR, :], EW[c][PR, qe:qe + 1], psS[PR, h * 64:(h + 1) * 64], ALU.mult, ALU.add, R=[ST[h], EW[c], psS], W=[ST[h]])
                PS.put(psS)
                yield
                CHP.put(XT, R1, AKm, RBm, RKm, Bht, Kht, Ut)
                YS, YQ = CHP.get(), CHP.get()
                O.copy(YS[R64, :], psY[R64, :], R=[psY], W=[YS], eng="dve")
                O.act(YQ[R64, :], psY[R64, :], AF.Square, R=[psY], W=[YQ])
                PS.put(psY)
                yield
                bg.append(rw_post(YS, YQ, Vt, qs, tok))

            def rw_post(YS, YQ, Vt, qs, tok):
                sm = SM.get()
                O.reduce(sm[R64, 0:8], YS[R64, :].rearrange("p (h v) -> p h v", v=64), ALU.add, R=[YS], W=[sm])
                yield
                O.reduce(sm[R64, 8:16], YQ[R64, :].rearrange("p (h v) -> p h v", v=64), ALU.add, R=[YQ, sm], W=[sm])
                yield
                O.ts(sm[R64, 16:24], sm[R64, 0:8], 1.0 / 64, None, ALU.mult, R=[sm], W=[sm])
                yield
                O.tt(sm[R64, 24:32], sm[R64, 16:24], sm[R64, 16:24], ALU.mult, R=[sm], W=[sm])
                yield
                O.stt(sm[R64, 32:40], sm[R64, 8:16], 1.0 / 64, sm[R64, 24:32], ALU.mult, ALU.subtract, R=[sm], W=[sm])
                yield
                O.act(sm[R64, 32:40], sm[R64, 32:40], AF.Sqrt, R=[sm], W=[sm], bias=self.cst("epsgn", 1, R64))
                yield
                O.recip(sm[R64, 32:40], sm[R64, 32:40], R=[sm], W=[sm])
                yield
                for h in range(8):
                    hc = slice(h * 64, (h + 1) * 64)
                    O.ts(YS[R64, hc], YS[R64, hc], sm[R64, 16 + h:17 + h], sm[R64, 32 + h:33 + h], ALU.subtract, ALU.mult, R=[YS, sm], W=[YS])
                    if h % 2:
                        yield
                O.tt(YS[R64, :], YS[R64, :], BCP[:, 0, :], ALU.mult, R=[YS], W=[YS])
                yield
                O.tt(YS[R64, :], YS[R64, :], BCP[:, 1, :], ALU.add, R=[YS], W=[YS])
                yield
                psk = PS.get()
                C.pe([O.mm(psk[R64, 2 * c:2 * c + 2], RK[c][:, qs], self.cst("HI", 2)) for c in range(4)], RK, [psk])
                O.copy(sm[R64, 40:48], psk[R64, 0:8], R=[psk, sm], W=[sm], eng="act")
                PS.put(psk)
                yield
                for h in range(8):
                    hc = slice(h * 64, (h + 1) * 64)
                    O.stt(YS[R64, hc], Vt[R64, hc], sm[R64, 40 + h:41 + h], YS[R64, hc], ALU.mult, ALU.add, R=[YS, Vt, sm], W=[YS])
                    if h % 2:
                        yield
                psg = PS.get()
                C.pe([O.mm(psg[R64, :], SLG[0][:, qs], LORA[:, 2, :], start=True, stop=False),
                      O.mm(psg[R64, :], SLG[1][:, qs], LORA[:, 3, :], start=False, stop=True)], SLG, [psg])
                YB = CHB.get()
                O.tt(YB[R64, :], YS[R64, :], psg[R64, :], ALU.mult, R=[YS, psg], W=[YB])
                PS.put(psg)
                yield
                yb_, yo_ = tok // YBLK, tok % YBLK
                C.dma("sp", y_blocks[yb_][yo_:yo_ + CH, 0:512], YB[R64, :], reads=[YB, y_regs[yb_]])
                CHB.put(YB)
                CHP.put(YS, YQ, Vt)
                SM.put(sm)


            def gd_chunk(q):
                qs = slice(q * CH, (q + 1) * CH)
                qe = (q + 1) * CH - 1
                tok = t0 + q * CH
                yb_, yo_ = tok // YBLK, tok % YBLK
                pst = PS.get()
                C.pe([O.tr(pst[R64, 0:64], GCR[R64, qs], IDENT[R64, 0:64])], [GCR], [pst])
                gcc = SM.get()
                O.copy(gcc[R64, 0:64], pst[R64, 0:64], R=[pst], W=[gcc], eng="act")
                PS.put(pst)
                yield
                DT = CHP.get()
                for h in range(4):
                    hc = slice(h * 64, (h + 1) * 64)
                    O.ts(DT[R64, hc], GCB[h][R64, qs], gcc[R64, 32 + h:33 + h], 0.0, ALU.subtract, ALU.min, R=[GCB[h], gcc], W=[DT])
                O.act(DT[R64, 0:256], DT[R64, 0:256], AF.Exp, R=[DT], W=[DT])
                DTS, DTI = CHP.get(), CHP.get()
                O.tt(DTS[R64, 0:256], DT[R64, 0:256], MU[:, 0:256], ALU.mult, R=[DT], W=[DTS])
                O.tt(DTI[R64, 0:256], DT[R64, 0:256], MUI[:, 0:256], ALU.mult, R=[DT], W=[DTI])
                SM.put(gcc)
                psL, psA = PS.get(), PS.get()
                C.pe([O.mm(psL[R64, slice(h * 64, (h + 1) * 64)], KN[h][:, qs], KB[h][:, qs]) for h in range(4)], KN + KB, [psL])
                C.pe([O.mm(psA[R64, slice(h * 64, (h + 1) * 64)], KN[h][:, qs], QN[h][:, qs]) for h in range(4)], KN + QN, [psA])
                Q0, AIm = CHP.get(), CHP.get()
                O.stt(Q0[R64, 0:256], psL[R64, 0:256], -1.0, DTS[R64, 0:256], ALU.mult, ALU.mult, R=[psL, DTS], W=[Q0])
                O.tt(AIm[R64, 0:256], psA[R64, 0:256], DTI[R64, 0:256], ALU.mult, R=[psA, DTI], W=[AIm])
                PS.put(psL, psA)
                yield
                CHP.put(DT, DTS, DTI)
                XT = yield from inverse(Q0, 4)
                psV, psK, psZ = PS.get(), PS.get(), PS.get()
                C.pe([O.tr(psV[R64, h * 128:(h + 1) * 128], VB[h][:, qs], IDENT) for h in range(4)], VB, [psV])
                C.pe([O.tr(psK[R64, h * 128:(h + 1) * 128], KD[h][:, qs], IDENT) for h in range(4)], KD, [psK])
                C.pe([O.tr(psZ[R64, h * 128:(h + 1) * 128], SZ[h][:, qs], IDENT) for h in range(4)], SZ, [psZ])
                VBt, KDt, SZt = CHP.get(), CHP.get(), CHP.get()
                O.copy(VBt[R64, :], psV[R64, :], R=[psV], W=[VBt], eng="act")
                O.copy(KDt[R64, :], psK[R64, :], R=[psK], W=[KDt], eng="dve")
                O.copy(SZt[R64, :], psZ[R64, :], R=[psZ], W=[SZt], eng="act")
                PS.put(psV, psK, psZ)
                yield
                psM = PS.get()
                C.pe([O.mm(psM[R64, h * 128:(h + 1) * 128], KBG[h][:, qs], SG_[h][:, :]) for h in range(4)], KBG + SG_, [psM])
                Dd = CHP.get()
                O.tt(Dd[R64, :], VBt[R64, :], psM[R64, :], ALU.subtract, R=[VBt, psM], W=[Dd])
                PS.put(psM)
                yield
                psVN = PS.get()
                C.pe([O.mm(psVN[R64, h * 128:(h + 1) * 128], XT[R64, h * 64:(h + 1) * 64], Dd[R64, h * 128:(h + 1) * 128]) for h in range(4)], [XT, Dd], [psVN])
                VNt = CHP.get()
                O.copy(VNt[R64, :], psVN[R64, :], R=[psVN], W=[VNt], eng="act")
                PS.put(psVN)
                yield
                psO = PS.get()
                f = []
                for h in range(4):
                    f.append(O.mm(psO[R64, h * 128:(h + 1) * 128], QD[h][:, qs], SG_[h][:, :], start=True, stop=False))
                    f.append(O.mm(psO[R64, h * 128:(h + 1) * 128], AIm[R64, h * 64:(h + 1) * 64], VNt[R64, h * 128:(h + 1) * 128], start=False, stop=True))
                C.pe(f, QD + SG_ + [AIm, VNt], [psO])
                psS = PS.get()
                C.pe([O.mm(psS[:, h * 128:(h + 1) * 128], KDt[R64, h * 128:(h + 1) * 128], VNt[R64, h * 128:(h + 1) * 128]) for h in range(4)], [KDt, VNt], [psS])
                for h in range(4):
                    O.stt(SG_[h][:, :], SG_[h][:, :], EGC[h][:, qe:qe + 1], psS[:, h * 128:(h + 1) * 128], ALU.mult, ALU.add, R=[SG_[h], EGC[h], psS], W=[SG_[h]])
                PS.put(psS)
                yield
                OS, OQ = CHP.get(), CHP.get()
                O.copy(OS[R64, :], psO[R64, :], R=[psO], W=[OS], eng="dve")
                O.act(OQ[R64, :], psO[R64, :], AF.Square, R=[psO], W=[OQ])
                PS.put(psO)
                yield
                CHP.put(XT, AIm, VBt, KDt, Dd, VNt)
                bg.append(gd_post(OS, OQ, SZt, yb_, yo_))

            def gd_post(OS, OQ, SZt, yb_, yo_):
                sm = SM.get()
                O.reduce(sm[R64, 0:4], OQ[R64, :].rearrange("p (h v) -> p h v", v=128), ALU.add, R=[OQ], W=[sm])
                yield
                O.act(sm[R64, 0:4], sm[R64, 0:4], AF.Sqrt, R=[sm], W=[sm], bias=self.cst("eps6", 1, R64), scale=1.0 / 128)
                yield
                O.recip(sm[R64, 0:4], sm[R64, 0:4], R=[sm], W=[sm])
                yield
                for h in range(4):
                    hc = slice(h * 128, (h + 1) * 128)
                    O.ts(OS[R64, hc], OS[R64, hc], sm[R64, h:h + 1], None, ALU.mult, R=[OS, sm], W=[OS])
                    if h % 2:
                        yield
                O.tt(OS[R64, :], OS[R64, :], BCP[:, 2, :], ALU.mult, R=[OS], W=[OS])
                yield
                YB = CHB.get()
                O.tt(YB[R64, :], OS[R64, :], SZt[R64, :], ALU.mult, R=[OS, SZt], W=[YB])
                yield
                C.dma("sp", y_blocks[yb_][yo_:yo_ + CH, 512:1024], YB[R64, :], reads=[YB, y_regs[yb_]])
                CHB.put(YB)
                CHP.put(SZt, OS, OQ)
                SM.put(sm)

            def stream(fn):
                for q in range(NCK):
                    yield from fn(q)

            gens = [stream(rw_chunk), stream(gd_chunk)]
            while gens or bg:
                for g_ in list(gens) + list(bg):
                    try:
                        next(g_)
                    except StopIteration:
                        (gens if g_ in gens else bg).remove(g_)
            FM.put(*(XV + RK + EW + Rt + At + Kh + Bh + SLG + [t for p_ in Kt + Bt for t in p_]))
            FM.put(*(QN + KN + KB + KBG + VB + QD + KD + SZ + GCB + EGC))
            if after_block is not None and (t0 + TP) % YBLK == 0:
                after_block((t0 + TP) // YBLK - 1)
            if after_pass is not None:
                after_pass(s, NT // TP)
        C.barrier()
        nc.sbuf_base, nc.sbuf_top = mark


TB = 512
GAIN_NAMES = ["mix_norm_pre", "mix_norm_post", "xa_norm_pre", "xa_norm_post", "mlp_norm_pre", "mlp_norm_post", "xa_norm_mem"]
NORM_EPS = 1e-6


def arr_w(W):
    K, N = W.shape
    return np.ascontiguousarray(W.reshape(K // 128, 128, N // 128, 128).transpose(2, 1, 0, 3))


class Big:
    def __init__(self, prog, cst_d, gains_d):
        self.P = prog
        nc, C, O = prog.nc, prog.C, prog.O
        self.nc, self.C, self.O, self.PS = nc, C, O, prog.PS
        self.mark = (nc.sbuf_base, nc.sbuf_top)
        self.ar = sb(nc, "arena", [128, 65536], BF16)
        self.g = [Tl(None, "g%d" % i) for i in range(128)]
        self.WB = Pool(nc, "wb", 3, [128, 32, 128], BF16, ctx=C)
        self.STG = Pool(nc, "stg", 6, [128, 512], F32, ctx=C)
        self.SQB = Pool(nc, "sqb", 3, [128, 512], BF16, ctx=C)
        self.ONESB = Tl(sb(nc, "onesb", [128, 128], BF16), "onesb")
        self.XS = Pool(nc, "xs", 2, [128, 1024], F32, ctx=C)
        self.YS = Pool(nc, "ysb", 2, [128, 1024], BF16, ctx=C)
        self.RS = Tl(sb(nc, "rs", [128, 512], F32), "rs")
        self.CS = Tl(sb(nc, "cs_b", [128, 260], F32), "cs_b")
        self.IDB = Tl(sb(nc, "idb", [128, 128], BF16), "idb")
        self.G = Tl(sb(nc, "gains", [128, 7, 32], F32), "gains")
        o = CST_OFF
        C.dma("sp", self.CS[:, 0:256], cst_d[:, o["ident"]:o["ident"] + 256], writes=[self.CS])
        C.dma("sp", self.CS[:, 256:260], cst_d[:, o["eps6"]:o["eps6"] + 4], writes=[self.CS])
        C.dma("sp", self.G[:, :, :], gains_d, writes=[self.G])
        O.copy(self.IDB[:, :], self.CS[:, 0:128], R=[self.CS], W=[self.IDB])
        O.copy(self.ONESB[:, :], self.CS[:, 128:256], R=[self.CS], W=[self.ONESB])
        C.barrier()
        self.IDENT = self.CS[:, 0:128]
        self.ONES = self.CS[:, 128:256]
        self.EPS = self.CS[:, 256:257]
        self.f32all = [self.ar[:, r * 32768:(r + 1) * 32768].bitcast(F32).rearrange("p (c t) -> p c t", t=512) for r in range(2)]
        self.bfall = [self.ar[:, q * 16384:(q + 1) * 16384].rearrange("p (c t) -> p c t", t=512) for q in range(4)]

    def close(self):
        self.C.barrier()
        self.nc.sbuf_base, self.nc.sbuf_top = self.mark

    def bf(self, q, j):
        return self.bfall[q][:, j, :], [self.g[q * 32 + j]]

    def f32(self, r, j):
        i = r * 64 + 2 * j
        return self.f32all[r][:, j, :], self.g[i:i + 2]

    def gain(self, name, j):
        k = GAIN_NAMES.index(name)
        return self.G[:, k, j:j + 1]

    def load_T(self, rows_ap, ntile, r, after=None):
        nc, C, O, PS = self.nc, self.C, self.O, self.PS
        for par in range(2):
          C.begin_stream()
          for i in range(ntile):
            for cp in range(par, 4, 2):
                xs = self.XS.get()
                C.dma("sp", xs[:, :], rows_ap[i * 128:(i + 1) * 128, cp * 1024:(cp + 1) * 1024], writes=[xs])
                for half in range(2):
                    ps = PS.get()
                    C.pe([O.tr(ps[:, k * 128:(k + 1) * 128], xs[:, (half * 4 + k) * 128:(half * 4 + k + 1) * 128], self.IDENT) for k in range(4)], [xs], [ps])
                    kc0 = cp * 8 + half * 4
                    tls = self.g[r * 64 + 2 * kc0:r * 64 + 2 * kc0 + 8]
                    O.copy(self.f32all[r][:, kc0:kc0 + 4, i * 128:(i + 1) * 128], ps[:, 0:512].rearrange("p (c t) -> p c t", t=128),
                           R=[ps], W=tls, eng=("act" if half else "dve"))
                    PS.put(ps)
                self.XS.put(xs)
        C.flush_streams([self.XS, PS])

    def rstd_of(self, r, T=TB):
        C, O, PS = self.C, self.O, self.PS
        acc = PS.get()
        for j in range(32):
            ap, tls = self.f32(r, j)
            sq = self.SQB.get()
            O.act(sq[:, 0:T], ap[:, 0:T], AF.Square, R=tls, W=[sq])
            C.pe([O.mm(acc[:, 0:T], self.ONESB[:, :], sq[:, 0:T], start=(j == 0), stop=(j == 31))], [sq], [acc])
            self.SQB.put(sq)
        O.act(self.RS[:, 0:T], acc[:, 0:T], AF.Ln, R=[acc], W=[self.RS], bias=self.EPS, scale=1.0 / 4096)
        PS.put(acc)
        O.act(self.RS[:, 0:T], self.RS[:, 0:T], AF.Exp, R=[self.RS], W=[self.RS], scale=-0.5)

    def prenorm(self, r, gname, q, T=TB):
        O = self.O
        self.rstd_of(r, T)
        for j in range(32):
            src, stl = self.f32(r, j)
            dst, dtl = self.bf(q, j)
            O.stt(dst[:, 0:T], src[:, 0:T], self.gain(gname, j), self.RS[:, 0:T], ALU.mult, ALU.mult, R=stl + [self.RS], W=dtl)

    def postnorm(self, r, gname, h_d, h_regs):
        C, O = self.C, self.O
        self.rstd_of(r)
        hjs = {}
        PRE = 4

        def issue(j):
            hj = self.STG.get()
            C.dma("sp", hj[:, :], h_d[j], reads=[h_regs[j]], writes=[hj])
            hjs[j] = hj

        for j in range(PRE):
            issue(j)
        for j in range(32):
            if j + PRE < 32:
                issue(j + PRE)
            src, stl = self.f32(r, j)
            hj = hjs.pop(j)
            O.stt(src, src, self.gain(gname, j), self.RS[:, :], ALU.mult, ALU.mult, R=stl + [self.RS], W=stl)
            O.tt(src, src, hj[:, :], ALU.add, R=stl + [hj], W=stl, eng="pool")
            self.STG.put(hj)
            C.dma("act", h_d[j], src, reads=stl, writes=[h_regs[j]])

    def wstream(self, jobs, pref=2):
        return WStream(self, jobs, pref)

    def chain(self, ws, ps, acts, T=TB, store_to=None):
        wt, KC = ws.next()
        if store_to is not None:
            self.C.dma("sp", store_to[0], wt[:, 0:KC, :], reads=[wt], writes=[store_to[1]])
        self.C.pe_steps([(self.O.mm(ps[:, 0:T], wt[:, kc, :], acts[kc][0][:, 0:T], start=(kc == 0), stop=(kc == KC - 1)), [wt] + list(acts[kc][1]))
                         for kc in range(KC)], [ps])
        self.WB.put(wt)


class WStream:
    def __init__(self, big, jobs, pref):
        self.b = big
        self.jobs = jobs
        self.pref = pref
        self.i = 0
        self.issued = 0
        self.q = []

    def _issue(self):
        job = self.jobs[self.issued]
        w_ap, KC = job[0], job[1]
        regs = job[2] if len(job) > 2 else []
        wt = self.b.WB.get()
        self.b.C.dma("pool", wt[:, 0:KC, :], w_ap, reads=regs, writes=[wt])
        self.q.append((wt, KC))
        self.issued += 1

    def next(self):
        while self.issued < min(self.i + 1 + self.pref, len(self.jobs)):
            self._issue()
        self.i += 1
        return self.q.pop(0)


def phase_A1(prog, cst_d, gains_d, x_d, NTOK, w_d, pT, pT_regs, w16=None):
    B = Big(prog, cst_d, gains_d)
    C, O, PS = B.C, B.O, B.PS
    z = B.STG.get()
    O.memset(z[:, :], 0.0, W=[z])
    for j in range(NCHA):
        C.dma("sp", pT[j, :, 0:4], z[:, 0:4], reads=[z], writes=[pT_regs[j]])
    B.STG.put(z)
    ws_all = B.wstream([(w_d[j], 32) for _ in range(NTOK // TB) for j in range(NCHA)])
    for s in range(NTOK // TB):
        t0 = s * TB
        B.load_T(x_d[t0:t0 + TB, :], TB // 128, 0)
        B.prenorm(0, "mix_norm_pre", 2)
        acts = [B.bf(2, kc) for kc in range(32)]
        if w16 is None:
            ws = ws_all
        elif s == 0:
            ws = B.wstream([(w_d[j], 32) for j in range(NCHA)])
        else:
            ws = B.wstream([(w16[0][j], 32, [w16[1][j]]) for j in range(NCHA)])
        for j in range(NCHA):
            ps = PS.get()
            B.chain(ws, ps, acts, store_to=((w16[0][j], w16[1][j]) if (w16 is not None and s == 0) else None))
            st = B.STG.get()
            O.copy(st[:, :], ps[:, :], R=[ps], W=[st], eng=("act" if j % 2 else "dve"))
            PS.put(ps)
            C.dma("sp", pT[j, :, 4 + t0:4 + t0 + TB], st[:, :], reads=[st], writes=[pT_regs[j]])
            B.STG.put(st)
    B.close()


def phase_B(prog, cst_d, gains_d, sel_d, x_d, mem_d, gat_blocks, gat_regs, NTB, W, h_d, out_d):
    B = Big(prog, cst_d, gains_d)
    nc, C, O, PS = B.nc, B.C, B.O, B.PS
    h_regs = [Tl(None, "h%d" % j) for j in range(32)]
    QT = Tl(sb(nc, "qT", [128, 4, 512], BF16), "qT")
    OT = Tl(sb(nc, "oT", [128, 4, 512], BF16), "oT")
    KT = Tl(sb(nc, "kT", [128, 4, 256], BF16), "kT")
    VT = Tl(sb(nc, "vT", [128, 2, 512], BF16), "vT")
    PRP = Pool(nc, "prp", 3, [128, 256], F32)
    PRB = Pool(nc, "prb", 3, [128, 256], BF16)
    PTB = Pool(nc, "ptb", 3, [128, 2, 128], BF16)
    SMB = Pool(nc, "smb", 4, [128, 4], F32)
    SCALE = float(128 ** -0.5)
    WR = W.get("_regs")

    def wj(k, j, KC):
        return (W[k][j], KC, [WR[k][j]]) if WR else (W[k][j], KC)

    YC = Pool(nc, "yc", 3, [128, 1024], BF16)
    SEL = Tl(sb(nc, "sel", [128, 4], F32), "sel")
    C.dma("sp", SEL[:, :], sel_d, writes=[SEL])
    C.barrier()

    B.load_T(mem_d, 2, 0)
    B.prenorm(0, "xa_norm_mem", 2, T=256)
    macts = [B.bf(2, kc) for kc in range(32)]
    ws = B.wstream([wj("w_k", j, 32) for j in range(4)])
    for h in range(4):
        ps = PS.get()
        B.chain(ws, ps, macts, T=256)
        O.copy(KT[:, h, :], ps[:, 0:256], R=[ps], W=[KT], eng="act")
        PS.put(ps)
    C.dma("pool", B.ar[:, 0:16384].rearrange("p (k n) -> p k n", n=512), W["w_v"], reads=(WR["w_v"] if WR else []), writes=B.g[0:32])
    for mt in range(2):
        ps = PS.get()
        C.pe([O.mm(ps[:, :], macts[kc][0][:, mt * 128:(mt + 1) * 128], B.bfall[0][:, kc, :], start=(kc == 0), stop=(kc == 31)) for kc in range(32)],
             B.g[0:32] + B.g[64:96], [ps])
        O.copy(VT[:, mt, :], ps[:, :], R=[ps], W=[VT], eng="dve")
        PS.put(ps)

    for s in range(NTB // TB):
        t0 = s * TB
        B.load_T(x_d[t0:t0 + TB, :], TB // 128, 0)
        for j in range(32):
            src, stl = B.f32(0, j)
            C.dma("sp", h_d[j], src, reads=stl, writes=[h_regs[j]])
        B.prenorm(0, "mix_norm_pre", 2)
        for r in range(4):
            for i in range(TB // 128):
                ys = B.YS.get()
                for k in range(4):
                    cand = YC.get()
                    blk = (k * NTB + s * TB) // YBLK
                    row = r * YBLK + i * 128
                    C.dma("sp", cand[:, :], gat_blocks[blk][row:row + 128, :], reads=[gat_regs[blk]], writes=[cand])
                    if k == 0:
                        O.ts(ys[:, :], cand[:, :], SEL[:, 0:1], None, ALU.mult, R=[cand], W=[ys])
                    else:
                        O.stt(ys[:, :], cand[:, :], SEL[:, k:k + 1], ys[:, :], ALU.mult, ALU.add, R=[cand, ys], W=[ys])
                    YC.put(cand)
                pb = PS.get()
                pbv = pb.t[:, :].bitcast(BF16)
                C.pe([O.tr(pbv[:, k * 128:(k + 1) * 128], ys[:, k * 128:(k + 1) * 128], B.IDB[:, :]) for k in range(8)], [ys], [pb])
                for part in range(2):
                    kc0 = part * 16 + r * 4
                    O.copy(B.bfall[3][:, kc0:kc0 + 4, i * 128:(i + 1) * 128], pbv[:, part * 512:(part + 1) * 512].rearrange("p (c t) -> p c t", t=128),
                           R=[pb], W=B.g[96 + kc0:96 + kc0 + 4], eng="dve")
                PS.put(pb)
                B.YS.put(ys)
        uacts = [B.bf(2, kc) for kc in range(32)]
        yrw = [B.bf(3, kc) for kc in range(16)]
        ygd = [B.bf(3, 16 + kc) for kc in range(16)]
        jobs = []
        for j in range(32):
            jobs += [wj("w_gate", j, 32), wj("w_gate", 32 + j, 32), wj("w_br_rw", j, 16), wj("w_br_gd", j, 16)]
        jobs += [wj("w_out", j, 32) for j in range(32)]
        jobs += [wj("w_q", j, 32) for j in range(4)]
        jobs += [wj("w_o", j, 4) for j in range(32)]
        for bi in range(8):
            jobs += [wj("w_up", bi * 16 + jj, 32) for jj in range(16)]
            jobs += [wj("w_down", bi * 32 + j, 16) for j in range(32)]
        ws = B.wstream(jobs)
        for j in range(32):
            p1, p2, p3, p4 = PS.get(), PS.get(), PS.get(), PS.get()
            B.chain(ws, p1, uacts)
            B.chain(ws, p2, uacts)
            B.chain(ws, p3, yrw)
            B.chain(ws, p4, ygd)
            s1, s2 = B.STG.get(), B.STG.get()
            O.act(s1[:, :], p1[:, :], AF.Sigmoid, R=[p1], W=[s1])
            O.act(s2[:, :], p2[:, :], AF.Sigmoid, R=[p2], W=[s2])
            O.tt(s1[:, :], s1[:, :], p3[:, :], ALU.mult, R=[s1, p3], W=[s1])
            O.tt(s2[:, :], s2[:, :], p4[:, :], ALU.mult, R=[s2, p4], W=[s2])
            dst, dtl = B.bf(0, j)
            O.tt(dst, s1[:, :], s2[:, :], ALU.add, R=[s1, s2], W=dtl, eng="pool")
            PS.put(p1, p2, p3, p4)
            B.STG.put(s1, s2)
        macts2 = [B.bf(0, kc) for kc in range(32)]
        for j in range(32):
            ps = PS.get()
            B.chain(ws, ps, macts2)
            dst, dtl = B.f32(1, j)
            O.copy(dst, ps[:, :], R=[ps], W=dtl, eng=("act" if j % 2 else "dve"))
            PS.put(ps)
        B.postnorm(1, "mix_norm_post", h_d, h_regs)
        B.prenorm(1, "xa_norm_pre", 0)
        cacts = [B.bf(0, kc) for kc in range(32)]
        for h in range(4):
            ps = PS.get()
            B.chain(ws, ps, cacts)
            O.copy(QT[:, h, :], ps[:, :], R=[ps], W=[QT], eng="act")
            PS.put(ps)
        for i in range(TB // 128):
            isl = slice(i * 128, (i + 1) * 128)
            for h in range(4):
                ps = PS.get()
                C.pe([O.mm(ps[:, 0:256], QT[:, h, isl], KT[:, h, :])], [QT, KT], [ps])
                sm = SMB.get()
                C.op("dve", lambda: nc.vector.tensor_reduce(out=sm[:, 0:1], in_=ps[:, 0:256], axis=AX.X, op=ALU.max), [ps], [sm])
                O.ts(sm[:, 1:2], sm[:, 0:1], -SCALE, None, ALU.mult, R=[sm], W=[sm])
                pr = PRP.get()
                O.act(pr[:, :], ps[:, 0:256], AF.Exp, R=[ps, sm], W=[pr, sm], bias=sm[:, 1:2], scale=SCALE, accum=sm[:, 2:3])
                PS.put(ps)
                O.recip(sm[:, 3:4], sm[:, 2:3], R=[sm], W=[sm])
                prb = PRB.get()
                O.ts(prb[:, :], pr[:, :], sm[:, 3:4], None, ALU.mult, R=[pr, sm], W=[prb])
                PRP.put(pr)
                SMB.put(sm)
                pb = PS.get()
                pbv = pb.t[:, :].bitcast(BF16)
                C.pe([O.tr(pbv[:, k * 128:(k + 1) * 128], prb[:, k * 128:(k + 1) * 128], B.IDB[:, :]) for k in range(2)], [prb], [pb])
                PRB.put(prb)
                pt = PTB.get()
                O.copy(pt[:, :, :], pbv[:, 0:256].rearrange("p (c t) -> p c t", t=128), R=[pb], W=[pt], eng="dve")
                PS.put(pb)
                po = PS.get()
                C.pe([O.mm(po[:, 0:128], VT[:, mc, h * 128:(h + 1) * 128], pt[:, mc, :], start=(mc == 0), stop=(mc == 1)) for mc in range(2)], [VT, pt], [po])
                PTB.put(pt)
                O.copy(OT[:, h, isl], po[:, 0:128], R=[po], W=[OT], eng="dve")
                PS.put(po)
        oacts = [(OT[:, h, :], [OT]) for h in range(4)]
        for j in range(32):
            ps = PS.get()
            B.chain(ws, ps, oacts)
            dst, dtl = B.f32(0, j)
            O.copy(dst, ps[:, :], R=[ps], W=dtl, eng=("act" if j % 2 else "dve"))
            PS.put(ps)
        B.postnorm(0, "xa_norm_post", h_d, h_regs)
        B.prenorm(0, "mlp_norm_pre", 2)
        facts = [B.bf(2, kc) for kc in range(32)]
        hacts = [B.bf(3, kc) for kc in range(16)]
        for bi in range(8):
            for jj in range(16):
                ps = PS.get()
                B.chain(ws, ps, facts)
                st = B.STG.get()
                O.act(st[:, :], ps[:, :], AF.Relu, R=[ps], W=[st])
                PS.put(ps)
                dst, dtl = hacts[jj]
                O.tt(dst, st[:, :], st[:, :], ALU.mult, R=[st], W=dtl, eng=("pool" if jj % 2 else "dve"))
                B.STG.put(st)
            for j in range(32):
                ps = PS.get()
                B.chain(ws, ps, hacts)
                dst, dtl = B.f32(0, j)
                if bi == 0:
                    O.copy(dst, ps[:, :], R=[ps], W=dtl, eng="act")
                else:
                    O.tt(dst, dst, ps[:, :], ALU.add, R=dtl + [ps], W=dtl)
                PS.put(ps)
        B.postnorm(0, "mlp_norm_post", h_d, h_regs)
        for i in range(TB // 128):
            for cp in range(4):
                xs = B.XS.get()
                for half in range(2):
                    ps = PS.get()
                    kc0 = cp * 8 + half * 4
                    tls = B.g[2 * kc0:2 * kc0 + 8]
                    C.pe([O.tr(ps[:, k * 128:(k + 1) * 128], B.f32all[0][:, kc0 + k, i * 128:(i + 1) * 128], B.IDENT) for k in range(4)], tls, [ps])
                    O.copy(xs[:, half * 512:(half + 1) * 512], ps[:, :], R=[ps], W=[xs], eng=("act" if half else "dve"))
                    PS.put(ps)
                row = s * TB + i * 128
                C.dma("sp", out_d[row:row + 128, cp * 1024:(cp + 1) * 1024], xs[:, :], reads=[xs])
                B.XS.put(xs)
    B.close()


NSEQ = 4096
NCORE = 8
PRECAST_A1 = True
PRECAST = False
GATE0 = 6592 + 8224

W_SHAPES = {
    "w_gate": [64, 128, 32, 128], "w_br_rw": [32, 128, 16, 128], "w_br_gd": [32, 128, 16, 128], "w_out": [32, 128, 32, 128],
    "w_q": [4, 128, 32, 128], "w_k": [4, 128, 32, 128], "w_v": [128, 32, 512], "w_o": [32, 128, 4, 128],
    "w_up": [128, 128, 32, 128], "w_down": [256, 128, 16, 128],
}


def build_program(nseq=NSEQ, phases="A1,A2,X,B", ydt=BF16):
    nc = bass.Bass("TRN2", target_bir_lowering=False)
    ntb = nseq // 4
    d = {}
    d["xb"] = nc.dram_tensor("xb", [nseq, 4096], F32, kind="ExternalInput").ap()
    d["xB"] = nc.dram_tensor("xB", [ntb, 4096], F32, kind="ExternalInput").ap()
    d["memb"] = nc.dram_tensor("memb", [256, 4096], F32, kind="ExternalInput").ap()
    d["cst"] = nc.dram_tensor("cst", [128, CST_N], F32, kind="ExternalInput").ap()
    d["gains"] = nc.dram_tensor("gains", [128, 7, 32], F32, kind="ExternalInput").ap()
    d["sel"] = nc.dram_tensor("sel", [128, 4], F32, kind="ExternalInput").ap()
    d["prm"] = nc.dram_tensor("prm", [128, PRM_N], F32, kind="ExternalInput").ap()
    d["bc"] = nc.dram_tensor("bc", [64, 3, 512], F32, kind="ExternalInput").ap()
    d["lora"] = nc.dram_tensor("lora", [128, 4, 512], F32, kind="ExternalInput").ap()
    d["w_inA"] = nc.dram_tensor("w_inA", [NCHA, 128, 32, 128], F32, kind="ExternalInput").ap()
    W = {}
    if "B" in phases:
        for k, shp in W_SHAPES.items():
            W[k] = nc.dram_tensor(k, shp, F32, kind="ExternalInput").ap()
    out = nc.dram_tensor("out", [ntb, 4096], F32, kind="ExternalOutput").ap()
    if "B" not in phases:
        dbg = nc.dram_tensor("dbg", [128, 1024], ydt, kind="ExternalOutput").ap()
    pT = nc.dram_tensor("pT_scr", [NCHA, 128, 4 + nseq], F32).ap()
    nblk = nseq // YBLK
    y_ts = [nc.dram_tensor("y_scr%d" % k, [YBLK, 1024], ydt) for k in range(nblk)]
    gat_ts = [nc.dram_tensor("gat_scr%d" % k, [4 * YBLK, 1024], ydt) for k in range(nblk)]
    y_regs = [Tl(None, "y%d" % k) for k in range(nblk)]
    gat_regs = [Tl(None, "gat%d" % k) for k in range(nblk)]
    h_d = nc.dram_tensor("h_scr", [32, 128, TB], F32).ap()
    P = Prog(nc)
    C = P.C
    pT_regs = [Tl(None, "pT%d" % j) for j in range(NCHA)]
    wA16 = (nc.dram_tensor("w_inA_b16", [NCHA, 128, 32, 128], BF16).ap(), [Tl(None, "wA%d" % j) for j in range(NCHA)])
    cast_jobs = []
    Wb = W
    if "B" in phases and PRECAST:
        Wb = {"_regs": {}}
        for k, shp in W_SHAPES.items():
            Wb[k] = nc.dram_tensor(k + "_b16", shp, BF16).ap()
            if k == "w_v":
                rs_ = [Tl(None, "w_v_r%d" % i) for i in range(4)]
                Wb["_regs"][k] = rs_
                cast_jobs += [(Wb[k][:, 8 * i:8 * (i + 1), :], W[k][:, 8 * i:8 * (i + 1), :], rs_[i]) for i in range(4)]
            else:
                Wb["_regs"][k] = [Tl(None, "%s_r%d" % (k, j)) for j in range(shp[0])]
                cast_jobs += [(Wb[k][j], W[k][j], Wb["_regs"][k][j]) for j in range(shp[0])]
    cast_state = {"i": 0}

    def cast_some(s, npass):
        n = len(cast_jobs)
        upto = n if s + 1 >= npass else (n * (s + 1)) // npass
        while cast_state["i"] < upto:
            dst, src, reg = cast_jobs[cast_state["i"]]
            C.dma("pool", dst, src, writes=[reg])
            cast_state["i"] += 1

    if "A1" in phases:
        phase_A1(P, d["cst"], d["gains"], d["xb"], nseq, d["w_inA"], pT, pT_regs, w16=(wA16 if PRECAST_A1 else None))
    if "A2" in phases:
        mark = (nc.sbuf_base, nc.sbuf_top)
        P.load_consts(d["cst"])
        state = {"prev": None}

        def exchange(k):
            C.deps("pool", [], [y_regs[k]])
            if state["prev"] is not None:
                C._wait("pool", state["prev"])
            key = ("cc", k)
            cc = nc.gpsimd.collective_compute("AllGather", ALU.bypass, replica_groups=[[0, 1, 2, 3], [4, 5, 6, 7]],
                                              ins=[y_ts[k].ap().opt()], outs=[gat_ts[k].ap().opt()])
            cc.then_inc(C._sem(key))
            ev = Ev(key, 1)
            gat_regs[k].w = ev
            state["prev"] = ev

        P.phase_A2(nseq, pT, pT_regs, d["prm"], d["bc"], d["lora"], [t.ap() for t in y_ts], y_regs,
                   after_block=(exchange if "X" in phases else None), ydt=ydt, after_pass=cast_some)
        nc.sbuf_base, nc.sbuf_top = mark
    if "B" in phases:
        cast_some(0, 1)
        phase_B(P, d["cst"], d["gains"], d["sel"], d["xB"], d["memb"], [t.ap() for t in gat_ts], gat_regs, ntb, Wb, h_d, out)
    if "B" not in phases:
        if "X" in phases:
            C.dma("sp", dbg, gat_ts[nblk - 1].ap()[3 * YBLK:3 * YBLK + 128, :], reads=[gat_regs[nblk - 1]])
        else:
            C.dma("sp", dbg, y_ts[0].ap()[0:128, :])
    C.barrier()
    return nc, P


_HOST_CACHE = {}


def prepare_inputs(inp, nseq=NSEQ):
    f = lambda a: np.ascontiguousarray(np.asarray(a, dtype=np.float32))
    w_in = np.asarray(inp["w_in"], dtype=np.float32)[0]
    shared = {
        "cst": make_consts(),
        "gains": np.ascontiguousarray(np.stack([np.asarray(inp[n], np.float32)[0].reshape(32, 128).T for n in GAIN_NAMES], axis=1)),
        "w_gate": arr_w(w_in[:, GATE0:GATE0 + 8192]),
        "w_br_rw": arr_w(f(inp["w_branch_rwkv"])[0]),
        "w_br_gd": arr_w(f(inp["w_branch_gdn"])[0]),
        "w_out": arr_w(f(inp["w_mix_out"])[0]),
        "w_q": arr_w(f(inp["xa_w_q"])[0]),
        "w_k": arr_w(f(inp["xa_w_kv"])[0][:, 0:512]),
        "w_v": np.ascontiguousarray(f(inp["xa_w_kv"])[0][:, 512:1024].reshape(32, 128, 512).transpose(1, 0, 2)),
        "w_o": arr_w(f(inp["xa_w_o"])[0]),
        "w_up": arr_w(f(inp["mlp_w_up"])[0]),
        "w_down": np.concatenate([arr_w(f(inp["mlp_w_down"])[0][bi * 2048:(bi + 1) * 2048, :]) for bi in range(8)], axis=0),
    }
    x = np.asarray(inp["x"], np.float32)
    mem = np.asarray(inp["mem"], np.float32)
    ntb = nseq // 4
    groups = {}
    for g in range(4):
        idx = mixer_col_index(g)
        wA = np.zeros((4096, NCHA * 128), np.float32)
        m = idx >= 0
        wA[:, m] = w_in[:, idx[m]]
        prm, bc, lora = pack_mixer_params(inp, g)
        sel = np.zeros((128, 4), np.float32)
        sel[:, g] = 1.0
        groups[g] = {"w_inA": arr_w(wA), "prm": prm, "bc": bc, "lora": lora, "sel": sel}
    in_maps = []
    for c in range(NCORE):
        b, g = c // 4, c % 4
        m_ = dict(shared)
        m_.update(groups[g])
        m_["xb"] = np.ascontiguousarray(x[b, 0:nseq])
        m_["xB"] = np.ascontiguousarray(x[b, g * ntb:(g + 1) * ntb])
        m_["memb"] = np.ascontiguousarray(mem[b])
        in_maps.append(m_)
    return in_maps


def kernel(**inputs):
    nc, _ = build_program()
    in_maps = prepare_inputs(inputs)
    res = run_bass_kernel_spmd(nc, in_maps, core_ids=list(range(NCORE)))
    out = np.zeros((2, NSEQ, 4096), np.float32)
    ntb = NSEQ // 4
    for c in range(NCORE):
        b, g = c // 4, c % 4
        out[b, g * ntb:(g + 1) * ntb] = np.asarray(res.results[c]["out"], np.float32)
    return out
```

```python
import numpy as np
import concourse.bass as bass
import concourse.mybir as mybir
from concourse.bass_utils import run_bass_kernel_spmd

F32 = mybir.dt.float32
BF16 = mybir.dt.bfloat16
AF = mybir.ActivationFunctionType
ALU = mybir.AluOpType
AX = mybir.AxisListType

SEM_WRAP = 30000
LAM = 0.6065306597126334


class Ev:
    __slots__ = ("key", "val")

    def __init__(self, key, val):
        self.key = key
        self.val = val


class Tl:
    def __init__(self, t, name, excl=False):
        self.t = t
        self.name = name
        self.w = None
        self.r = []
        self.excl = excl

    def __getitem__(self, idx):
        return self.t[idx]


class Deferred:
    def __init__(self, st):
        self.st = st
        self.idx = [0] * len(st)
        self.left = sum(len(l) for l in st)

    def step(self, n=1):
        while n > 0 and self.left > 0:
            for k, lst in enumerate(self.st):
                if self.idx[k] < len(lst):
                    lst[self.idx[k]]()
                    self.idx[k] += 1
                    self.left -= 1
                    n -= 1
                    if n <= 0:
                        break

    def finish(self, pools=()):
        self.step(self.left)
        for p in pools:
            p.end_streams()


class Ctx:
    _SID = 0

    def __init__(self, nc):
        self.nc = nc
        self.E = {"pe": nc.tensor, "dve": nc.vector, "act": nc.scalar, "pool": nc.gpsimd, "sp": nc.sync}
        self.cnt = {k: 0 for k in self.E}
        self.sems = {}
        self.waited = {k: {} for k in self.E}
        self.dmacnt = {}
        self.ndma = 0
        self.nsem = 0
        self.ninst = 0
        self.limit = 10 ** 9
        self.streams = None
        self.cur = -1
        self.trace = None

    def _sem(self, key):
        s = self.sems.get(key)
        if s is None:
            s = self.nc.alloc_semaphore(name="s%d" % self.nsem)
            self.nsem += 1
            self.sems[key] = s
        return s

    def _wait(self, eng, ev):
        if ev is None:
            return
        w = self.waited[eng]
        if w.get(ev.key, 0) >= ev.val:
            return
        self.E[eng].wait_ge(self._sem(ev.key), ev.val)
        w[ev.key] = ev.val

    def deps(self, eng, reads, writes):
        for r in reads:
            if r.w is not None:
                self._wait(eng, r.w)
        for r in writes:
            if r.w is not None:
                self._wait(eng, r.w)
            for e in r.r:
                self._wait(eng, e)

    def commit(self, ev, reads, writes):
        for r in writes:
            r.w = ev
            r.r = []
        for r in reads:
            if r in writes:
                continue
            r.r.append(ev)
            if len(r.r) > 10:
                d = {}
                for e in r.r:
                    if e.key not in d or d[e.key].val < e.val:
                        d[e.key] = e
                r.r = list(d.values())

    def _tick(self, eng):
        if self.trace is not None:
            import sys
            f = sys._getframe(2)
            ln = []
            while f is not None and len(ln) < 4:
                ln.append(f.f_lineno)
                f = f.f_back
            self.trace.append((self.ninst, eng, ln))
        self.cnt[eng] += 1
        c = self.cnt[eng]
        blk, val = divmod(c, SEM_WRAP)
        if val == 0:
            blk -= 1
            val = SEM_WRAP
        return Ev((eng, blk), val)

    def begin_stream(self):
        if self.streams is None:
            self.streams = []
            self.sids = []
        self.streams.append([])
        Ctx._SID += 1
        self.sids.append(Ctx._SID)
        self.cur = len(self.streams) - 1

    @property
    def sid(self):
        return self.sids[self.cur]

    def take_streams(self):
        st, self.streams, self.cur = self.streams, None, -1
        return Deferred(st)

    def flush_streams(self, pools=()):
        st, self.streams, self.cur = self.streams, None, -1
        idx = [0] * len(st)
        live = True
        while live:
            live = False
            for k, lst in enumerate(st):
                if idx[k] < len(lst):
                    lst[idx[k]]()
                    idx[k] += 1
                    live = True
        for p in pools:
            p.end_streams()

    def op(self, eng, fn, reads=(), writes=()):
        if self.streams is not None:
            reads, writes = list(reads), list(writes)
            self.streams[self.cur].append(lambda: self._op(eng, fn, reads, writes))
            return None
        return self._op(eng, fn, reads, writes)

    def _op(self, eng, fn, reads=(), writes=()):
        if self.ninst >= self.limit:
            return None
        if any(r.excl for r in reads):
            writes = list(writes) + [r for r in reads if r.excl and r not in writes]
        self.deps(eng, reads, writes)
        ins = fn()
        ev = self._tick(eng)
        ins.then_inc(self._sem(ev.key), 1)
        self.commit(ev, reads, writes)
        self.ninst += 1
        return ev

    def pe(self, fns, reads=(), writes=()):
        if self.streams is not None:
            fns, reads, writes = list(fns), list(reads), list(writes)
            self.streams[self.cur].append(lambda: self._pe(fns, reads, writes))
            return None
        return self._pe(fns, reads, writes)

    def pe_steps(self, steps, writes=()):
        if self.streams is not None:
            steps, writes = list(steps), list(writes)
            self.streams[self.cur].append(lambda: self._pe_steps(steps, writes))
            return None
        return self._pe_steps(steps, writes)

    def _pe_steps(self, steps, writes=()):
        if self.ninst >= self.limit:
            return None
        self.deps("pe", [], writes)
        ins = None
        allr = []
        for fn, reads in steps:
            self.deps("pe", reads, [])
            ins = fn()
            for r in reads:
                if r not in allr:
                    allr.append(r)
        ev = self._tick("pe")
        ins.then_inc(self._sem(ev.key), 1)
        self.commit(ev, allr, writes)
        self.ninst += len(steps)
        return ev

    def _pe(self, fns, reads=(), writes=()):
        if self.ninst >= self.limit:
            return None
        if any(r.excl for r in reads):
            writes = list(writes) + [r for r in reads if r.excl and r not in writes]
        self.deps("pe", reads, writes)
        ins = None
        for fn in fns:
            ins = fn()
        ev = self._tick("pe")
        ins.then_inc(self._sem(ev.key), 1)
        self.commit(ev, reads, writes)
        self.ninst += len(fns)
        return ev

    def dma(self, q, out, in_, reads=(), writes=(), **kw):
        if self.streams is not None:
            reads, writes = list(reads), list(writes)
            self.streams[self.cur].append(lambda: self._dma(q, out, in_, reads, writes, **kw))
            return None
        return self._dma(q, out, in_, reads, writes, **kw)

    def _dma(self, q, out, in_, reads=(), writes=(), **kw):
        if self.ninst >= self.limit:
            return None
        self.deps(q, reads, writes)
        semkey = ("dma", self.ndma % 32)
        self.ndma += 1
        prev = self.dmacnt.get(semkey, 0)
        if prev:
            self._wait(q, Ev(semkey, prev))
        c = prev + 16
        self.dmacnt[semkey] = c
        ins = self.E[q].dma_start(out=out, in_=in_, **kw)
        ins.then_inc(self._sem(semkey), 16)
        ev = Ev(semkey, c)
        self.commit(ev, reads, writes)
        self.ninst += 1
        return ev

    def wait_all(self, eng, regs):
        for r in regs:
            if r.w is not None:
                self._wait(eng, r.w)
            for e in r.r:
                self._wait(eng, e)

    def barrier(self):
        evs = []
        for eng, c in self.cnt.items():
            if c:
                blk, val = divmod(c, SEM_WRAP)
                if val == 0:
                    blk -= 1
                    val = SEM_WRAP
                evs.append(Ev((eng, blk), val))
        for k, c in self.dmacnt.items():
            evs.append(Ev(k, c))
        for eng in self.E:
            for ev in evs:
                self._wait(eng, ev)


_UID = [0]


def sb(nc, name, shape, dtype):
    _UID[0] += 1
    return nc.alloc_sbuf_tensor("%s_u%d" % (name, _UID[0]), shape, dtype)


def pb_(nc, name, shape, dtype):
    _UID[0] += 1
    return nc.alloc_psum_tensor("%s_u%d" % (name, _UID[0]), shape, dtype)


class Pool:
    def __init__(self, nc, name, n, shape, dtype, psum=False, ctx=None):
        self.ctx = ctx
        self.sfree = {}
        self.owned = {}
        self.quota = None
        self.free = []
        for i in range(n):
            if psum:
                t = pb_(nc, "%s%d" % (name, i), shape, dtype)
            else:
                t = sb(nc, "%s%d" % (name, i), shape, dtype)
            self.free.append(Tl(t, "%s%d" % (name, i), excl=psum))

    def get(self):
        c = self.ctx
        if c is not None and c.streams is not None:
            l = self.sfree.get(c.sid)
            n = self.owned.get(c.sid, 0)
            if l and not (self.quota is not None and n < self.quota and self.free):
                return l.pop(0)
            self.owned[c.sid] = n + 1
        return self.free.pop(0)

    def put(self, *ts):
        c = self.ctx
        if c is not None and c.streams is not None:
            self.sfree.setdefault(c.sid, []).extend(ts)
            return
        for t in ts:
            self.free.append(t)

    def end_streams(self):
        for l in self.sfree.values():
            self.free.extend(l)
        self.sfree = {}
        self.owned = {}


class Ops:
    def __init__(self, C):
        self.C = C
        self.nc = C.nc

    def ts(self, out, in0, s1, s2, op0, op1=None, R=(), W=(), eng="dve"):
        e = self.nc.vector if eng == "dve" else self.nc.gpsimd
        if op1 is None:
            return self.C.op(eng, lambda: e.tensor_scalar(out=out, in0=in0, scalar1=s1, scalar2=None, op0=op0), R, W)
        return self.C.op(eng, lambda: e.tensor_scalar(out=out, in0=in0, scalar1=s1, scalar2=s2, op0=op0, op1=op1), R, W)

    def stt(self, out, in0, s, in1, op0, op1, R=(), W=()):
        return self.C.op("dve", lambda: self.nc.vector.scalar_tensor_tensor(out=out, in0=in0, scalar=s, in1=in1, op0=op0, op1=op1), R, W)

    def tt(self, out, in0, in1, op, R=(), W=(), eng="dve"):
        e = self.nc.vector if eng == "dve" else self.nc.gpsimd
        return self.C.op(eng, lambda: e.tensor_tensor(out=out, in0=in0, in1=in1, op=op), R, W)

    def act(self, out, in_, func, R=(), W=(), bias=None, scale=None, accum=None):
        kw = {}
        if bias is not None:
            kw["bias"] = bias
        if scale is not None:
            kw["scale"] = scale
        if accum is not None:
            kw["accum_out"] = accum
        return self.C.op("act", lambda: self.nc.scalar.activation(out=out, in_=in_, func=func, **kw), R, W)

    def copy(self, out, in_, R=(), W=(), eng="dve"):
        if eng == "act":
            return self.C.op("act", lambda: self.nc.scalar.copy(out=out, in_=in_), R, W)
        e = self.nc.vector if eng == "dve" else self.nc.gpsimd
        return self.C.op(eng, lambda: e.tensor_copy(out=out, in_=in_), R, W)

    def recip(self, out, in_, R=(), W=()):
        return self.C.op("dve", lambda: self.nc.vector.reciprocal(out=out, in_=in_), R, W)

    def reduce(self, out, in_, op, R=(), W=()):
        return self.C.op("dve", lambda: self.nc.vector.tensor_reduce(out=out, in_=in_, axis=AX.X, op=op), R, W)

    def scan(self, out, d0, d1, R=(), W=()):
        return self.C.op("dve", lambda: self.nc.vector.tensor_tensor_scan(out=out, data0=d0, data1=d1, initial=0.0, op0=ALU.mult, op1=ALU.add), R, W)

    def memset(self, ap, val, W=(), eng="dve"):
        e = self.nc.vector if eng == "dve" else self.nc.gpsimd
        return self.C.op(eng, lambda: e.memset(ap, val), (), W)

    def mm(self, out, lhsT, rhs, start=True, stop=True):
        return lambda: self.nc.tensor.matmul(out, lhsT, rhs, start=start, stop=stop)

    def tr(self, out, in_, ident):
        return lambda: self.nc.tensor.transpose(out, in_, ident)


def _layout(items):
    off = {}
    n = 0
    for name, w in items:
        off[name] = n
        n += w
    return off, n


CST_OFF, CST_N = _layout([
    ("ident", 128), ("ones", 128), ("bones", 128), ("MU", 512), ("MUI", 512), ("I8", 512), ("HI", 2),
    ("SelB", 512), ("SelG", 512), ("eps6", 1), ("epsgn", 1), ("one", 1), ("zero", 1),
])


def make_consts():
    c = np.zeros((128, CST_N), np.float32)
    o = CST_OFF
    c[:, o["ident"]:o["ident"] + 128] = np.eye(128, dtype=np.float32)
    c[:, o["ones"]:o["ones"] + 128] = 1.0
    c[0:64, o["bones"]:o["bones"] + 64] = 1.0
    c[64:128, o["bones"] + 64:o["bones"] + 128] = 1.0
    s = np.arange(64)[:, None]
    t = np.arange(64)[None, :]
    for h in range(8):
        c[0:64, o["MU"] + h * 64:o["MU"] + (h + 1) * 64] = (t > s)
        c[0:64, o["MUI"] + h * 64:o["MUI"] + (h + 1) * 64] = (t >= s)
        c[0:64, o["I8"] + h * 64:o["I8"] + (h + 1) * 64] = (t == s)
    c[0:64, o["HI"]] = 1.0
    c[64:128, o["HI"] + 1] = 1.0
    for h in range(4):
        c[h, o["SelB"] + h * 128:o["SelB"] + (h + 1) * 128] = 1.0
        c[32 + h, o["SelG"] + h * 128:o["SelG"] + (h + 1) * 128] = 1.0
    c[:, o["eps6"]] = 1e-6
    c[:, o["epsgn"]] = 64e-5
    c[:, o["one"]] = 1.0
    return c


_prm_items = []
for _n in ("mix_r", "mix_k", "mix_v"):
    _prm_items += [("%s%d" % (_n, c), 1) for c in range(4)]
_prm_items += [("mix_lw", 1), ("mix_la", 1), ("mix_lg0", 1), ("mix_lg1", 1)]
for _n in ("w0_", "a0_", "kk_", "ka_", "rk_"):
    _prm_items += [("%s%d" % (_n, c), 1) for c in range(4)]
_prm_items += [("cw%d_%d" % (j, i), 1) for j in range(4) for i in range(12)]
_prm_items += [("dtb", 1), ("alog", 1)]
PRM_OFF, PRM_N = _layout(_prm_items)
MIX_NAMES = ["mix_r%d" % c for c in range(4)] + ["mix_k%d" % c for c in range(4)] + ["mix_v%d" % c for c in range(4)] + [
    "mix_lw", "mix_la", "mix_lg0", "mix_lg1"]
DRV_OFF, DRV_N = _layout([("om_" + n, 1) for n in MIX_NAMES] + [("omka_%d" % c, 1) for c in range(4)] + [("nea", 1)])

NCHA = 33


def pack_mixer_params(inp, g):
    P = np.zeros((128, PRM_N), np.float32)
    sm = inp["rwkv_shift_mix"][0]
    ch0 = g * 512
    for c in range(4):
        sl = slice(ch0 + c * 128, ch0 + (c + 1) * 128)
        P[:, PRM_OFF["mix_r%d" % c]] = sm[0 * 2048:][sl]
        P[:, PRM_OFF["mix_k%d" % c]] = sm[1 * 2048:][sl]
        P[:, PRM_OFF["mix_v%d" % c]] = sm[2 * 2048:][sl]
        P[:, PRM_OFF["w0_%d" % c]] = inp["rwkv_w0"][0][sl]
        P[:, PRM_OFF["a0_%d" % c]] = inp["rwkv_a0"][0][sl]
        P[:, PRM_OFF["kk_%d" % c]] = inp["rwkv_k_k"][0][sl]
        P[:, PRM_OFF["ka_%d" % c]] = inp["rwkv_k_a"][0][sl]
        P[:, PRM_OFF["rk_%d" % c]] = inp["rwkv_r_k"][0].reshape(-1)[sl]
    P[0:96, PRM_OFF["mix_lw"]] = sm[6144:6240]
    P[0:96, PRM_OFF["mix_la"]] = sm[6240:6336]
    P[:, PRM_OFF["mix_lg0"]] = sm[6336:6464]
    P[:, PRM_OFF["mix_lg1"]] = sm[6464:6592]
    cw = inp["gdn_conv_w"][0]
    for j in range(4):
        for part in range(3):
            for h in range(4):
                col = part * 2048 + (4 * g + h) * 128
                P[:, PRM_OFF["cw%d_%d" % (j, part * 4 + h)]] = cw[j, col:col + 128]
    P[32:36, PRM_OFF["dtb"]] = inp["gdn_dt_bias"][0][4 * g:4 * g + 4]
    P[32:36, PRM_OFF["alog"]] = inp["gdn_a_log"][0][4 * g:4 * g + 4]
    bc = np.zeros((64, 3, 512), np.float32)
    bc[:, 0, :] = inp["rwkv_gn_w"][0][ch0:ch0 + 512][None, :]
    bc[:, 1, :] = inp["rwkv_gn_b"][0][ch0:ch0 + 512][None, :]
    bc[:, 2, :] = np.tile(inp["gdn_norm_w"][0], 4)[None, :]
    lora = np.zeros((128, 4, 512), np.float32)
    lora[0:96, 0, :] = inp["rwkv_w_up"][0][:, ch0:ch0 + 512]
    lora[0:96, 1, :] = inp["rwkv_a_up"][0][:, ch0:ch0 + 512]
    lora[:, 2, :] = inp["rwkv_g_up"][0][0:128, ch0:ch0 + 512]
    lora[:, 3, :] = inp["rwkv_g_up"][0][128:256, ch0:ch0 + 512]
    return P, bc, lora


def mixer_col_index(g):
    idx = -np.ones(NCHA * 128, np.int64)
    RW = 2048
    for part in range(3):
        for c in range(4):
            j = part * 4 + c
            idx[j * 128:(j + 1) * 128] = part * RW + g * 512 + c * 128 + np.arange(128)
    idx[12 * 128:12 * 128 + 96] = 3 * RW + np.arange(96)
    idx[13 * 128:13 * 128 + 96] = 3 * RW + 96 + np.arange(96)
    idx[14 * 128:16 * 128] = 3 * RW + 192 + np.arange(256)
    base = 6592
    for part in range(4):
        for h in range(4):
            j = 16 + part * 4 + h
            idx[j * 128:(j + 1) * 128] = base + part * 2048 + (4 * g + h) * 128 + np.arange(128)
    for h in range(4):
        idx[32 * 128 + h] = base + 8192 + 4 * g + h
        idx[32 * 128 + 32 + h] = base + 8192 + 16 + 4 * g + h
    return idx


TP = 256
YBLK = 512
CH = 64
NCK = TP // CH


class Prog:
    def __init__(self, nc):
        self.nc = nc
        self.C = Ctx(nc)
        self.O = Ops(self.C)
        self.PS = Pool(nc, "ps", 8, [128, 512], F32, psum=True, ctx=self.C)

    def load_consts(self, cst_d):
        nc, C = self.nc, self.C
        self.CST = Tl(sb(nc, "sb_cst", [128, CST_N], F32), "cst")
        C.dma("sp", self.CST[:, :], cst_d, writes=[self.CST])

    def cst(self, name, w, rows=slice(0, 128)):
        o = CST_OFF[name]
        return self.CST[rows, o:o + w]

    def phase_A2(self, NT, pT, pT_regs, prm_d, bc_d, lora_d, y_blocks, y_regs, after_block=None, ydt=BF16, after_pass=None):
        nc, C, O, PS = self.nc, self.C, self.O, self.PS
        mark = (nc.sbuf_base, nc.sbuf_top)
        FM = Pool(nc, "fm", 102, [128, TP + 4], F32, ctx=C)
        CHP = Pool(nc, "chp", 24, [64, 512], F32)
        CH16 = Pool(nc, "ch16", 14, [64, 512], BF16)
        CHB = Pool(nc, "chb", 4, [64, 512], ydt)
        SM = Pool(nc, "sm", 8, [128, 64], F32)
        PRM = Tl(sb(nc, "sb_prm", [128, PRM_N], F32), "prm")
        DRV = Tl(sb(nc, "drv", [128, DRV_N], F32), "drv")
        BCP = Tl(sb(nc, "sb_bcp", [64, 3, 512], F32), "bcp")
        LORA = Tl(sb(nc, "sb_lora", [128, 4, 512], F32), "lora")
        BG = Tl(sb(nc, "bg", [128, TP], F32), "bg")
        GCR = Tl(sb(nc, "gcr", [128, TP], F32), "gcr")
        E1 = Tl(sb(nc, "e1", [128, TP], F32), "e1")
        ST = [Tl(sb(nc, "strw%d" % i, [128, 64], F32), "strw%d" % i) for i in range(8)]
        SG_ = [Tl(sb(nc, "stgd%d" % i, [128, 128], F32), "stgd%d" % i) for i in range(4)]
        C.dma("sp", PRM[:, :], prm_d, writes=[PRM])
        C.dma("sp", BCP[:, :, :], bc_d, writes=[BCP])
        C.dma("sp", LORA[:, :, :], lora_d, writes=[LORA])
        for t in ST + SG_:
            O.memset(t[:, :], 0.0, W=[t])
        O.memset(BG[:, :], 0.0, W=[BG])
        O.memset(GCR[:, :], 0.0, W=[GCR])
        O.memset(E1[:, :], 0.0, W=[E1])
        for n in MIX_NAMES:
            a, b = PRM_OFF[n], DRV_OFF["om_" + n]
            O.ts(DRV[:, b:b + 1], PRM[:, a:a + 1], -1.0, 1.0, ALU.mult, ALU.add, R=[PRM], W=[DRV])
        for c in range(4):
            a, b = PRM_OFF["ka_%d" % c], DRV_OFF["omka_%d" % c]
            O.ts(DRV[:, b:b + 1], PRM[:, a:a + 1], -1.0, 1.0, ALU.mult, ALU.add, R=[PRM], W=[DRV])
        a, b = PRM_OFF["alog"], DRV_OFF["nea"]
        O.act(DRV[:, b:b + 1], PRM[:, a:a + 1], AF.Exp, R=[PRM, DRV], W=[DRV])
        O.ts(DRV[:, b:b + 1], DRV[:, b:b + 1], -1.0, None, ALU.mult, R=[DRV], W=[DRV])
        C.barrier()

        def pc(name, rows=slice(0, 128)):
            o = PRM_OFF[name]
            return PRM[rows, o:o + 1]

        def dc(name, rows=slice(0, 128)):
            o = DRV_OFF[name]
            return DRV[rows, o:o + 1]

        IDENT = self.cst("ident", 128)
        ONES = self.cst("ones", 128)
        BONES = self.cst("bones", 128)
        EPS6 = self.cst("eps6", 1)
        EPSGN = self.cst("epsgn", 1)
        W_ = slice(0, TP)
        R64 = slice(0, 64)

        def mul(a, b, rows=slice(0, 128), eng="dve"):
            o = FM.get()
            O.tt(o[rows, W_], a[rows, W_], b[rows, W_], ALU.mult, R=[a, b], W=[o], eng=eng)
            return o

        def inverse(Q0, nh):
            Wd = nh * 64
            hcs = [slice(h * 64, (h + 1) * 64) for h in range(nh)]
            ps = PS.get()
            C.pe([O.tr(ps[R64, hc], Q0[R64, hc], IDENT[R64, 0:64]) for hc in hcs], [Q0], [ps])
            QT = CH16.get()
            O.copy(QT[R64, 0:Wd], ps[R64, 0:Wd], R=[ps], W=[QT], eng="act")
            PS.put(ps)
            Q = CH16.get()
            O.copy(Q[R64, 0:Wd], Q0[R64, 0:Wd], R=[Q0], W=[Q], eng="pool")
            XT = CH16.get()
            O.tt(XT[R64, 0:Wd], Q0[R64, 0:Wd], self.cst("I8", Wd, R64), ALU.add, R=[Q0], W=[XT])
            CHP.put(Q0)
            yield
            for j in range(1, 6):
                ps1 = PS.get()
                C.pe([O.mm(ps1[R64, hc], Q[R64, hc], QT[R64, hc]) for hc in hcs], [Q, QT], [ps1])
                QTn = CH16.get()
                O.copy(QTn[R64, 0:Wd], ps1[R64, 0:Wd], R=[ps1], W=[QTn], eng="act")
                PS.put(ps1)
                Qn = None
                if j < 5:
                    ps2 = PS.get()
                    C.pe([O.mm(ps2[R64, hc], QT[R64, hc], Q[R64, hc]) for hc in hcs], [Q, QT], [ps2])
                    Qn = CH16.get()
                    O.copy(Qn[R64, 0:Wd], ps2[R64, 0:Wd], R=[ps2], W=[Qn], eng="act")
                    PS.put(ps2)
                yield
                ps3 = PS.get()
                C.pe([O.mm(ps3[R64, hc], QTn[R64, hc], XT[R64, hc]) for hc in hcs], [QTn, XT], [ps3])
                XTn = CH16.get() if j < 5 else CHP.get()
                O.tt(XTn[R64, 0:Wd], XT[R64, 0:Wd], ps3[R64, 0:Wd], ALU.add, R=[XT, ps3], W=[XTn])
                PS.put(ps3)
                CH16.put(Q, QT, XT)
                Q, QT, XT = Qn, QTn, XTn
                yield
            CH16.put(QT)
            return XT

        MU = self.cst("MU", 512, R64)
        MUI = self.cst("MUI", 512, R64)

        for s in range(NT // TP):
            t0 = s * TP

            def load_p(j, q="sp"):
                t = FM.get()
                C.dma(q, t[:, 0:TP + 4], pT[j, :, t0:t0 + TP + 4], reads=[pT_regs[j]], writes=[t])
                return t

            def xmix(P, name, rows=slice(0, 128)):
                t1 = FM.get()
                xm = FM.get()
                O.ts(t1[rows, W_], P[rows, 3:3 + TP], pc(name, rows), None, ALU.mult, R=[P], W=[t1], eng="pool")
                O.stt(xm[rows, W_], P[rows, 4:4 + TP], dc("om_" + name, rows), t1[rows, W_], ALU.mult, ALU.add, R=[P, t1], W=[xm])
                FM.put(t1, P)
                return xm

            R96 = slice(0, 96)
            XLW = xmix(load_p(12), "mix_lw", R96)
            O.act(XLW[R96, W_], XLW[R96, W_], AF.Tanh, R=[XLW], W=[XLW])
            XLA = xmix(load_p(13), "mix_la", R96)
            SLG = []
            for kc in range(2):
                x = xmix(load_p(14 + kc), "mix_lg%d" % kc)
                O.act(x[:, W_], x[:, W_], AF.Sigmoid, R=[x], W=[x])
                SLG.append(x)
            XV, RK, EW, Rt, At, Kt, Bt, Kh, Bh = ([None] * 4 for _ in range(9))
            for c in range(4):
                C.begin_stream()
                csl = slice(c * 128, (c + 1) * 128)
                XR = xmix(load_p(0 + c), "mix_r%d" % c)
                XK = xmix(load_p(4 + c), "mix_k%d" % c)
                XV[c] = xmix(load_p(8 + c), "mix_v%d" % c)
                ps = PS.get()
                C.pe([O.mm(ps[:, W_], LORA[R96, 0, csl], XLW[R96, W_])], [XLW], [ps])
                SG = FM.get()
                O.act(SG[:, W_], ps[:, W_], AF.Sigmoid, R=[ps], W=[SG], bias=pc("w0_%d" % c))
                PS.put(ps)
                ps = PS.get()
                C.pe([O.mm(ps[:, W_], LORA[R96, 1, csl], XLA[R96, W_])], [XLA], [ps])
                AA = FM.get()
                O.act(AA[:, W_], ps[:, W_], AF.Sigmoid, R=[ps], W=[AA], bias=pc("a0_%d" % c))
                PS.put(ps)
                KX = FM.get()
                O.ts(KX[:, W_], XK[:, W_], pc("kk_%d" % c), None, ALU.mult, R=[XK], W=[KX], eng="pool")
                SQ = FM.get()
                O.act(SQ[:, W_], KX[:, W_], AF.Square, R=[KX], W=[SQ])
                ps = PS.get()
                C.pe([O.mm(ps[:, W_], BONES, SQ[:, W_])], [SQ], [ps])
                RN = FM.get()
                O.act(RN[:, W_], ps[:, W_], AF.Ln, R=[ps], W=[RN], bias=EPS6)
                PS.put(ps)
                FM.put(SQ)
                O.act(RN[:, W_], RN[:, W_], AF.Exp, R=[RN], W=[RN], scale=-0.5)
                KK = mul(KX, RN, eng="pool")
                FM.put(KX, RN)
                T1 = FM.get()
                O.ts(T1[:, W_], AA[:, W_], pc("ka_%d" % c), dc("omka_%d" % c), ALU.mult, ALU.add, R=[AA], W=[T1])
                KP = mul(XK, T1, eng="pool")
                Bv = mul(KK, AA, eng="pool")
                FM.put(XK, T1, AA)
                rk = FM.get()
                O.stt(rk[:, W_], XR[:, W_], pc("rk_%d" % c), KP[:, W_], ALU.mult, ALU.mult, R=[XR, KP], W=[rk])
                RK[c] = rk
                CS = FM.get()
                for q in range(NCK):
                    qs = slice(q * CH, (q + 1) * CH)
                    O.scan(CS[:, qs], ONES[:, 0:CH], SG[:, qs], R=[SG], W=[CS])
                ew = FM.get()
                O.act(ew[:, W_], CS[:, W_], AF.Exp, R=[CS], W=[ew], scale=-LAM)
                EW[c] = ew
                EWI = FM.get()
                O.act(EWI[:, W_], CS[:, W_], AF.Exp, R=[CS], W=[EWI], scale=LAM)
                CSX = FM.get()
                O.tt(CSX[:, W_], CS[:, W_], SG[:, W_], ALU.subtract, R=[CS, SG], W=[CSX], eng="pool")
                O.act(CSX[:, W_], CSX[:, W_], AF.Exp, R=[CSX], W=[CSX], scale=-LAM)
                FM.put(SG)
                DF = FM.get()
                for q in range(NCK):
                    qs = slice(q * CH, (q + 1) * CH)
                    e = (q + 1) * CH - 1
                    O.ts(DF[:, qs], CS[:, qs], CS[:, e:e + 1], None, ALU.subtract, R=[CS], W=[DF])
                O.act(DF[:, W_], DF[:, W_], AF.Exp, R=[DF], W=[DF], scale=LAM)
                FM.put(CS)
                Rt[c] = mul(XR, ew)
                FM.put(XR)
                a_ = FM.get()
                O.stt(a_[:, W_], KK[:, W_], -1.0, CSX[:, W_], ALU.mult, ALU.mult, R=[KK, CSX], W=[a_])
                At[c] = a_
                FM.put(KK, CSX)
                km, bm = [], []
                for hh in range(2):
                    hi = self.CST[:, CST_OFF["HI"] + hh:CST_OFF["HI"] + hh + 1]
                    k_ = FM.get()
                    O.stt(k_[:, W_], KP[:, W_], hi, EWI[:, W_], ALU.mult, ALU.mult, R=[KP, EWI], W=[k_])
                    b_ = FM.get()
                    O.stt(b_[:, W_], Bv[:, W_], hi, EWI[:, W_], ALU.mult, ALU.mult, R=[Bv, EWI], W=[b_])
                    km.append(k_)
                    bm.append(b_)
                Kt[c] = km
                Bt[c] = bm
                Kh[c] = mul(KP, DF, eng="pool")
                Bh[c] = mul(Bv, DF, eng="pool")
                FM.put(KP, Bv, EWI, DF)
            C.flush_streams([FM, PS])
            FM.put(XLW, XLA)

            def conv(j, widx):
                P = load_p(j)
                acc = FM.get()
                O.ts(acc[:, W_], P[:, 1:1 + TP], pc("cw0_%d" % widx), None, ALU.mult, R=[P], W=[acc])
                for k in range(1, 4):
                    O.stt(acc[:, W_], P[:, 1 + k:1 + k + TP], pc("cw%d_%d" % (k, widx)), acc[:, W_], ALU.mult, ALU.add, R=[P, acc], W=[acc])
                O.act(acc[:, W_], acc[:, W_], AF.Silu, R=[acc], W=[acc])
                FM.put(P)
                return acc

            def l2n(X, scale=None):
                SQ = FM.get()
                O.act(SQ[:, W_], X[:, W_], AF.Square, R=[X], W=[SQ])
                ps = PS.get()
                C.pe([O.mm(ps[:, W_], ONES, SQ[:, W_])], [SQ], [ps])
                RN = FM.get()
                O.act(RN[:, W_], ps[:, W_], AF.Ln, R=[ps], W=[RN], bias=EPS6)
                PS.put(ps)
                O.act(RN[:, W_], RN[:, W_], AF.Exp, R=[RN], W=[RN], scale=-0.5)
                o = FM.get()
                if scale is None:
                    O.tt(o[:, W_], X[:, W_], RN[:, W_], ALU.mult, R=[X, RN], W=[o])
                else:
                    O.stt(o[:, W_], X[:, W_], scale, RN[:, W_], ALU.mult, ALU.mult, R=[X, RN], W=[o])
                FM.put(SQ, RN, X)
                return o

            Pba = load_p(32)
            R4 = slice(0, 4)
            RA = slice(32, 36)
            O.act(BG[R4, W_], Pba[R4, 4:4 + TP], AF.Sigmoid, R=[Pba, BG], W=[BG])
            O.act(E1[RA, W_], Pba[RA, 4:4 + TP], AF.Exp, R=[Pba, E1], W=[E1], bias=pc("dtb", RA))
            O.act(E1[RA, W_], E1[RA, W_], AF.Ln, R=[E1], W=[E1], bias=self.cst("one", 1, RA))
            O.ts(BG[RA, W_], E1[RA, W_], dc("nea", RA), None, ALU.mult, R=[E1, BG], W=[BG])
            for q in range(NCK):
                qs = slice(q * CH, (q + 1) * CH)
                O.scan(GCR[RA, qs], ONES[RA, 0:CH], BG[RA, qs], R=[BG, GCR], W=[GCR])
            FM.put(Pba)
            QN, KN, KB, KBG, VB, QD, KD, SZ, GCB, EGC = [], [], [], [], [], [], [], [], [], []
            for h in range(4):
                C.begin_stream()
                qn = l2n(conv(16 + h, 0 + h), scale=float(128 ** -0.5))
                kn = l2n(conv(20 + h, 4 + h))
                cv = conv(24 + h, 8 + h)
                Pz = load_p(28 + h)
                sz = FM.get()
                O.act(sz[:, W_], Pz[:, 4:4 + TP], AF.Silu, R=[Pz], W=[sz])
                FM.put(Pz)
                ps = PS.get()
                C.pe([O.mm(ps[:, W_], self.CST[R64, CST_OFF["SelB"] + h * 128:CST_OFF["SelB"] + (h + 1) * 128], BG[R64, W_])], [BG], [ps])
                beta = FM.get()
                O.copy(beta[:, W_], ps[:, W_], R=[ps], W=[beta], eng="act")
                PS.put(ps)
                ps = PS.get()
                C.pe([O.mm(ps[:, W_], self.CST[R64, CST_OFF["SelG"] + h * 128:CST_OFF["SelG"] + (h + 1) * 128], GCR[R64, W_])], [GCR], [ps])
                gcb = FM.get()
                O.copy(gcb[:, W_], ps[:, W_], R=[ps], W=[gcb], eng="dve")
                egc = FM.get()
                O.act(egc[:, W_], ps[:, W_], AF.Exp, R=[ps], W=[egc])
                PS.put(ps)
                kb = mul(kn, beta, eng="pool")
                kbg = mul(kb, egc, eng="pool")
                vb = mul(cv, beta, eng="pool")
                qd = mul(qn, egc)
                DF = FM.get()
                for q in range(NCK):
                    qs = slice(q * CH, (q + 1) * CH)
                    e = (q + 1) * CH - 1
                    O.ts(DF[:, qs], gcb[:, qs], gcb[:, e:e + 1], -1.0, ALU.subtract, ALU.mult, R=[gcb], W=[DF])
                O.act(DF[:, W_], DF[:, W_], AF.Exp, R=[DF], W=[DF])
                kd = mul(kn, DF, eng="pool")
                FM.put(cv, beta, DF)
                QN.append(qn); KN.append(kn); KB.append(kb); KBG.append(kbg); VB.append(vb)
                QD.append(qd); KD.append(kd); SZ.append(sz); GCB.append(gcb); EGC.append(egc)
                if h == 3:
                    C.flush_streams([FM, PS])

            bg = []

            def rw_chunk(q):
                qs = slice(q * CH, (q + 1) * CH)
                qe = (q + 1) * CH - 1
                tok = t0 + q * CH
                psV, psB, psK = PS.get(), PS.get(), PS.get()
                C.pe([O.tr(psV[R64, c * 128:(c + 1) * 128], XV[c][:, qs], IDENT) for c in range(4)], XV, [psV])
                C.pe([O.tr(psB[R64, c * 128:(c + 1) * 128], Bh[c][:, qs], IDENT) for c in range(4)], Bh, [psB])
                C.pe([O.tr(psK[R64, c * 128:(c + 1) * 128], Kh[c][:, qs], IDENT) for c in range(4)], Kh, [psK])
                Vt, Bht, Kht = CHP.get(), CHP.get(), CHP.get()
                O.copy(Vt[R64, :], psV[R64, :], R=[psV], W=[Vt], eng="act")
                O.copy(Bht[R64, :], psB[R64, :], R=[psB], W=[Bht], eng="act")
                O.copy(Kht[R64, :], psK[R64, :], R=[psK], W=[Kht], eng="act")
                PS.put(psV, psB, psK)
                yield
                psN, psAK, psRB, psRK = PS.get(), PS.get(), PS.get(), PS.get()
                fN, fAK, fRB, fRK = [], [], [], []
                for h in range(8):
                    c, hh = h // 2, h % 2
                    PR = slice(hh * 64, hh * 64 + 64)
                    hc = slice(h * 64, (h + 1) * 64)
                    fN.append(O.mm(psN[R64, hc], Bt[c][hh][:, qs], At[c][:, qs]))
                    fAK.append(O.mm(psAK[R64, hc], Kt[c][hh][:, qs], At[c][:, qs]))
                    fRB.append(O.mm(psRB[R64, hc], Bt[c][hh][:, qs], Rt[c][:, qs]))
                    fRK.append(O.mm(psRK[R64, hc], Kt[c][hh][:, qs], Rt[c][:, qs]))
                Btf = [t for p_ in Bt for t in p_]
                Ktf = [t for p_ in Kt for t in p_]
                C.pe(fN, Btf + At, [psN])
                C.pe(fAK, Ktf + At, [psAK])
                C.pe(fRB, Btf + Rt, [psRB])
                C.pe(fRK, Ktf + Rt, [psRK])
                Q0, AKm, RBm, RKm = CHP.get(), CHP.get(), CHP.get(), CHP.get()
                O.tt(Q0[R64, :], psN[R64, :], MU, ALU.mult, R=[psN], W=[Q0])
                O.tt(AKm[R64, :], psAK[R64, :], MU, ALU.mult, R=[psAK], W=[AKm])
                O.tt(RBm[R64, :], psRB[R64, :], MUI, ALU.mult, R=[psRB], W=[RBm])
                O.tt(RKm[R64, :], psRK[R64, :], MUI, ALU.mult, R=[psRK], W=[RKm])
                PS.put(psN, psAK, psRB, psRK)
                yield
                XT = yield from inverse(Q0, 8)
                psR1 = PS.get()
                f = []
                for h in range(8):
                    c, hh = h // 2, h % 2
                    PR = slice(hh * 64, hh * 64 + 64)
                    hc = slice(h * 64, (h + 1) * 64)
                    f.append(O.mm(psR1[R64, hc], At[c][:, qs], ST[h][:, :], start=True, stop=False))
                    f.append(O.mm(psR1[R64, hc], AKm[R64, hc], Vt[R64, hc], start=False, stop=True))
                C.pe(f, At + ST + [AKm, Vt], [psR1])
                R1 = CHP.get()
                O.copy(R1[R64, :], psR1[R64, :], R=[psR1], W=[R1], eng="act")
                PS.put(psR1)
                yield
                psU = PS.get()
                C.pe([O.mm(psU[R64, slice(h * 64, (h + 1) * 64)], XT[R64, slice(h * 64, (h + 1) * 64)], R1[R64, slice(h * 64, (h + 1) * 64)]) for h in range(8)], [XT, R1], [psU])
                Ut = CHP.get()
                O.copy(Ut[R64, :], psU[R64, :], R=[psU], W=[Ut], eng="act")
                PS.put(psU)
                yield
                psY = PS.get()
                f = []
                for h in range(8):
                    c, hh = h // 2, h % 2
                    PR = slice(hh * 64, hh * 64 + 64)
                    hc = slice(h * 64, (h + 1) * 64)
                    f.append(O.mm(psY[R64, hc], Rt[c][:, qs], ST[h][:, :], start=True, stop=False))
                    f.append(O.mm(psY[R64, hc], RBm[R64, hc], Ut[R64, hc], start=False, stop=False))
                    f.append(O.mm(psY[R64, hc], RKm[R64, hc], Vt[R64, hc], start=False, stop=True))
                C.pe(f, Rt + ST + [RBm, RKm, Ut, Vt], [psY])
                psS = PS.get()
                f = []
                for c in range(4):
                    cs_ = slice(c * 128, (c + 1) * 128)
                    f.append(O.mm(psS[:, cs_], Bht[R64, cs_], Ut[R64, cs_], start=True, stop=False))
                    f.append(O.mm(psS[:, cs_], Kht[R64, cs_], Vt[R64, cs_], start=False, stop=True))
                C.pe(f, [Bht, Kht, Ut, Vt], [psS])
                for h in range(8):
                    c, hh = h // 2, h % 2
                    PR = slice(hh * 64, hh * 64 + 64)
                    O.stt(ST[h][PR, :], ST[h][PR, :], EW[c][PR, qe:qe + 1], psS[PR, h * 64:(h + 1) * 64], ALU.mult, ALU.add, R=[ST[h], EW[c], psS], W=[ST[h]])
                PS.put(psS)
                yield
                CHP.put(XT, R1, AKm, RBm, RKm, Bht, Kht, Ut)
                YS, YQ = CHP.get(), CHP.get()
                O.copy(YS[R64, :], psY[R64, :], R=[psY], W=[YS], eng="dve")
                O.act(YQ[R64, :], psY[R64, :], AF.Square, R=[psY], W=[YQ])
                PS.put(psY)
                yield
                bg.append(rw_post(YS, YQ, Vt, qs, tok))

            def rw_post(YS, YQ, Vt, qs, tok):
                sm = SM.get()
                O.reduce(sm[R64, 0:8], YS[R64, :].rearrange("p (h v) -> p h v", v=64), ALU.add, R=[YS], W=[sm])
                yield
                O.reduce(sm[R64, 8:16], YQ[R64, :].rearrange("p (h v) -> p h v", v=64), ALU.add, R=[YQ, sm], W=[sm])
                yield
                O.ts(sm[R64, 16:24], sm[R64, 0:8], 1.0 / 64, None, ALU.mult, R=[sm], W=[sm])
                yield
                O.tt(sm[R64, 24:32], sm[R64, 16:24], sm[R64, 16:24], ALU.mult, R=[sm], W=[sm])
                yield
                O.stt(sm[R64, 32:40], sm[R64, 8:16], 1.0 / 64, sm[R64, 24:32], ALU.mult, ALU.subtract, R=[sm], W=[sm])
                yield
                O.act(sm[R64, 32:40], sm[R64, 32:40], AF.Sqrt, R=[sm], W=[sm], bias=self.cst("epsgn", 1, R64))
                yield
                O.recip(sm[R64, 32:40], sm[R64, 32:40], R=[sm], W=[sm])
                yield
                for h in range(8):
                    hc = slice(h * 64, (h + 1) * 64)
                    O.ts(YS[R64, hc], YS[R64, hc], sm[R64, 16 + h:17 + h], sm[R64, 32 + h:33 + h], ALU.subtract, ALU.mult, R=[YS, sm], W=[YS])
                    if h % 2:
                        yield
                O.tt(YS[R64, :], YS[R64, :], BCP[:, 0, :], ALU.mult, R=[YS], W=[YS])
                yield
                O.tt(YS[R64, :], YS[R64, :], BCP[:, 1, :], ALU.add, R=[YS], W=[YS])
                yield
                psk = PS.get()
                C.pe([O.mm(psk[R64, 2 * c:2 * c + 2], RK[c][:, qs], self.cst("HI", 2)) for c in range(4)], RK, [psk])
                O.copy(sm[R64, 40:48], psk[R64, 0:8], R=[psk, sm], W=[sm], eng="act")
                PS.put(psk)
                yield
                for h in range(8):
                    hc = slice(h * 64, (h + 1) * 64)
                    O.stt(YS[R64, hc], Vt[R64, hc], sm[R64, 40 + h:41 + h], YS[R64, hc], ALU.mult, ALU.add, R=[YS, Vt, sm], W=[YS])
                    if h % 2:
                        yield
                psg = PS.get()
                C.pe([O.mm(psg[R64, :], SLG[0][:, qs], LORA[:, 2, :], start=True, stop=False),
                      O.mm(psg[R64, :], SLG[1][:, qs], LORA[:, 3, :], start=False, stop=True)], SLG, [psg])
                YB = CHB.get()
                O.tt(YB[R64, :], YS[R64, :], psg[R64, :], ALU.mult, R=[YS, psg], W=[YB])
                PS.put(psg)
                yield
                yb_, yo_ = tok // YBLK, tok % YBLK
                C.dma("sp", y_blocks[yb_][yo_:yo_ + CH, 0:512], YB[R64, :], reads=[YB, y_regs[yb_]])
                CHB.put(YB)
                CHP.put(YS, YQ, Vt)
                SM.put(sm)


            def gd_chunk(q):
                qs = slice(q * CH, (q + 1) * CH)
                qe = (q + 1) * CH - 1
                tok = t0 + q * CH
                yb_, yo_ = tok // YBLK, tok % YBLK
                pst = PS.get()
                C.pe([O.tr(pst[R64, 0:64], GCR[R64, qs], IDENT[R64, 0:64])], [GCR], [pst])
                gcc = SM.get()
                O.copy(gcc[R64, 0:64], pst[R64, 0:64], R=[pst], W=[gcc], eng="act")
                PS.put(pst)
                yield
                DT = CHP.get()
                for h in range(4):
                    hc = slice(h * 64, (h + 1) * 64)
                    O.ts(DT[R64, hc], GCB[h][R64, qs], gcc[R64, 32 + h:33 + h], 0.0, ALU.subtract, ALU.min, R=[GCB[h], gcc], W=[DT])
                O.act(DT[R64, 0:256], DT[R64, 0:256], AF.Exp, R=[DT], W=[DT])
                DTS, DTI = CHP.get(), CHP.get()
                O.tt(DTS[R64, 0:256], DT[R64, 0:256], MU[:, 0:256], ALU.mult, R=[DT], W=[DTS])
                O.tt(DTI[R64, 0:256], DT[R64, 0:256], MUI[:, 0:256], ALU.mult, R=[DT], W=[DTI])
                SM.put(gcc)
                psL, psA = PS.get(), PS.get()
                C.pe([O.mm(psL[R64, slice(h * 64, (h + 1) * 64)], KN[h][:, qs], KB[h][:, qs]) for h in range(4)], KN + KB, [psL])
                C.pe([O.mm(psA[R64, slice(h * 64, (h + 1) * 64)], KN[h][:, qs], QN[h][:, qs]) for h in range(4)], KN + QN, [psA])
                Q0, AIm = CHP.get(), CHP.get()
                O.stt(Q0[R64, 0:256], psL[R64, 0:256], -1.0, DTS[R64, 0:256], ALU.mult, ALU.mult, R=[psL, DTS], W=[Q0])
                O.tt(AIm[R64, 0:256], psA[R64, 0:256], DTI[R64, 0:256], ALU.mult, R=[psA, DTI], W=[AIm])
                PS.put(psL, psA)
                yield
                CHP.put(DT, DTS, DTI)
                XT = yield from inverse(Q0, 4)
                psV, psK, psZ = PS.get(), PS.get(), PS.get()
                C.pe([O.tr(psV[R64, h * 128:(h + 1) * 128], VB[h][:, qs], IDENT) for h in range(4)], VB, [psV])
                C.pe([O.tr(psK[R64, h * 128:(h + 1) * 128], KD[h][:, qs], IDENT) for h in range(4)], KD, [psK])
                C.pe([O.tr(psZ[R64, h * 128:(h + 1) * 128], SZ[h][:, qs], IDENT) for h in range(4)], SZ, [psZ])
                VBt, KDt, SZt = CHP.get(), CHP.get(), CHP.get()
                O.copy(VBt[R64, :], psV[R64, :], R=[psV], W=[VBt], eng="act")
                O.copy(KDt[R64, :], psK[R64, :], R=[psK], W=[KDt], eng="act")
                O.copy(SZt[R64, :], psZ[R64, :], R=[psZ], W=[SZt], eng="act")
                PS.put(psV, psK, psZ)
                yield
                psM = PS.get()
                C.pe([O.mm(psM[R64, h * 128:(h + 1) * 128], KBG[h][:, qs], SG_[h][:, :]) for h in range(4)], KBG + SG_, [psM])
                Dd = CHP.get()
                O.tt(Dd[R64, :], VBt[R64, :], psM[R64, :], ALU.subtract, R=[VBt, psM], W=[Dd])
                PS.put(psM)
                yield
                psVN = PS.get()
                C.pe([O.mm(psVN[R64, h * 128:(h + 1) * 128], XT[R64, h * 64:(h + 1) * 64], Dd[R64, h * 128:(h + 1) * 128]) for h in range(4)], [XT, Dd], [psVN])
                VNt = CHP.get()
                O.copy(VNt[R64, :], psVN[R64, :], R=[psVN], W=[VNt], eng="act")
                PS.put(psVN)
                yield
                psO = PS.get()
                f = []
                for h in range(4):
                    f.append(O.mm(psO[R64, h * 128:(h + 1) * 128], QD[h][:, qs], SG_[h][:, :], start=True, stop=False))
                    f.append(O.mm(psO[R64, h * 128:(h + 1) * 128], AIm[R64, h * 64:(h + 1) * 64], VNt[R64, h * 128:(h + 1) * 128], start=False, stop=True))
                C.pe(f, QD + SG_ + [AIm, VNt], [psO])
                psS = PS.get()
                C.pe([O.mm(psS[:, h * 128:(h + 1) * 128], KDt[R64, h * 128:(h + 1) * 128], VNt[R64, h * 128:(h + 1) * 128]) for h in range(4)], [KDt, VNt], [psS])
                for h in range(4):
                    O.stt(SG_[h][:, :], SG_[h][:, :], EGC[h][:, qe:qe + 1], psS[:, h * 128:(h + 1) * 128], ALU.mult, ALU.add, R=[SG_[h], EGC[h], psS], W=[SG_[h]])
                PS.put(psS)
                yield
                OS, OQ = CHP.get(), CHP.get()
                O.copy(OS[R64, :], psO[R64, :], R=[psO], W=[OS], eng="dve")
                O.act(OQ[R64, :], psO[R64, :], AF.Square, R=[psO], W=[OQ])
                PS.put(psO)
                yield
                CHP.put(XT, AIm, VBt, KDt, Dd, VNt)
                bg.append(gd_post(OS, OQ, SZt, yb_, yo_))

            def gd_post(OS, OQ, SZt, yb_, yo_):
                sm = SM.get()
                O.reduce(sm[R64, 0:4], OQ[R64, :].rearrange("p (h v) -> p h v", v=128), ALU.add, R=[OQ], W=[sm])
                yield
                O.act(sm[R64, 0:4], sm[R64, 0:4], AF.Sqrt, R=[sm], W=[sm], bias=self.cst("eps6", 1, R64), scale=1.0 / 128)
                yield
                O.recip(sm[R64, 0:4], sm[R64, 0:4], R=[sm], W=[sm])
                yield
                for h in range(4):
                    hc = slice(h * 128, (h + 1) * 128)
                    O.ts(OS[R64, hc], OS[R64, hc], sm[R64, h:h + 1], None, ALU.mult, R=[OS, sm], W=[OS])
                    if h % 2:
                        yield
                O.tt(OS[R64, :], OS[R64, :], BCP[:, 2, :], ALU.mult, R=[OS], W=[OS])
                yield
                YB = CHB.get()
                O.tt(YB[R64, :], OS[R64, :], SZt[R64, :], ALU.mult, R=[OS, SZt], W=[YB])
                yield
                C.dma("sp", y_blocks[yb_][yo_:yo_ + CH, 512:1024], YB[R64, :], reads=[YB, y_regs[yb_]])
                CHB.put(YB)
                CHP.put(SZt, OS, OQ)
                SM.put(sm)

            def stream(fn):
                for q in range(NCK):
                    yield from fn(q)

            gens = [stream(rw_chunk), stream(gd_chunk)]
            while gens or bg:
                for g_ in list(gens) + list(bg):
                    try:
                        next(g_)
                    except StopIteration:
                        (gens if g_ in gens else bg).remove(g_)
            FM.put(*(XV + RK + EW + Rt + At + Kh + Bh + SLG + [t for p_ in Kt + Bt for t in p_]))
            FM.put(*(QN + KN + KB + KBG + VB + QD + KD + SZ + GCB + EGC))
            if after_block is not None and (t0 + TP) % YBLK == 0:
                after_block((t0 + TP) // YBLK - 1)
            if after_pass is not None:
                after_pass(s, NT // TP)
        C.barrier()
        nc.sbuf_base, nc.sbuf_top = mark


TB = 512
GAIN_NAMES = ["mix_norm_pre", "mix_norm_post", "xa_norm_pre", "xa_norm_post", "mlp_norm_pre", "mlp_norm_post", "xa_norm_mem"]
NORM_EPS = 1e-6


def arr_w(W):
    K, N = W.shape
    return np.ascontiguousarray(W.reshape(K // 128, 128, N // 128, 128).transpose(2, 1, 0, 3))


class Big:
    def __init__(self, prog, cst_d, gains_d):
        self.P = prog
        nc, C, O = prog.nc, prog.C, prog.O
        self.nc, self.C, self.O, self.PS = nc, C, O, prog.PS
        self.mark = (nc.sbuf_base, nc.sbuf_top)
        self.ar = sb(nc, "arena", [128, 65536], BF16)
        self.g = [Tl(None, "g%d" % i) for i in range(128)]
        self.WB = Pool(nc, "wb", 3, [128, 32, 128], BF16, ctx=C)
        self.STG = Pool(nc, "stg", 6, [128, 512], F32, ctx=C)
        self.SQB = Pool(nc, "sqb", 3, [128, 512], BF16, ctx=C)
        self.ONESB = Tl(sb(nc, "onesb", [128, 128], BF16), "onesb")
        self.XS = Pool(nc, "xs", 2, [128, 1024], F32, ctx=C)
        self.YS = Pool(nc, "ysb", 2, [128, 1024], BF16, ctx=C)
        self.RS = Tl(sb(nc, "rs", [128, 512], F32), "rs")
        self.CS = Tl(sb(nc, "cs_b", [128, 260], F32), "cs_b")
        self.IDB = Tl(sb(nc, "idb", [128, 128], BF16), "idb")
        self.G = Tl(sb(nc, "gains", [128, 7, 32], F32), "gains")
        o = CST_OFF
        C.dma("sp", self.CS[:, 0:256], cst_d[:, o["ident"]:o["ident"] + 256], writes=[self.CS])
        C.dma("sp", self.CS[:, 256:260], cst_d[:, o["eps6"]:o["eps6"] + 4], writes=[self.CS])
        C.dma("sp", self.G[:, :, :], gains_d, writes=[self.G])
        O.copy(self.IDB[:, :], self.CS[:, 0:128], R=[self.CS], W=[self.IDB])
        O.copy(self.ONESB[:, :], self.CS[:, 128:256], R=[self.CS], W=[self.ONESB])
        C.barrier()
        self.IDENT = self.CS[:, 0:128]
        self.ONES = self.CS[:, 128:256]
        self.EPS = self.CS[:, 256:257]
        self.f32all = [self.ar[:, r * 32768:(r + 1) * 32768].bitcast(F32).rearrange("p (c t) -> p c t", t=512) for r in range(2)]
        self.bfall = [self.ar[:, q * 16384:(q + 1) * 16384].rearrange("p (c t) -> p c t", t=512) for q in range(4)]

    def close(self):
        self.C.barrier()
        self.nc.sbuf_base, self.nc.sbuf_top = self.mark

    def bf(self, q, j):
        return self.bfall[q][:, j, :], [self.g[q * 32 + j]]

    def f32(self, r, j):
        i = r * 64 + 2 * j
        return self.f32all[r][:, j, :], self.g[i:i + 2]

    def gain(self, name, j):
        k = GAIN_NAMES.index(name)
        return self.G[:, k, j:j + 1]

    def load_T(self, rows_ap, ntile, r, after=None):
        nc, C, O, PS = self.nc, self.C, self.O, self.PS
        for par in range(2):
          C.begin_stream()
          for i in range(ntile):
            for cp in range(par, 4, 2):
                xs = self.XS.get()
                C.dma("sp", xs[:, :], rows_ap[i * 128:(i + 1) * 128, cp * 1024:(cp + 1) * 1024], writes=[xs])
                for half in range(2):
                    ps = PS.get()
                    C.pe([O.tr(ps[:, k * 128:(k + 1) * 128], xs[:, (half * 4 + k) * 128:(half * 4 + k + 1) * 128], self.IDENT) for k in range(4)], [xs], [ps])
                    kc0 = cp * 8 + half * 4
                    tls = self.g[r * 64 + 2 * kc0:r * 64 + 2 * kc0 + 8]
                    O.copy(self.f32all[r][:, kc0:kc0 + 4, i * 128:(i + 1) * 128], ps[:, 0:512].rearrange("p (c t) -> p c t", t=128),
                           R=[ps], W=tls, eng=("act" if half else "dve"))
                    PS.put(ps)
                self.XS.put(xs)
        C.flush_streams([self.XS, PS])

    def rstd_of(self, r, T=TB):
        C, O, PS = self.C, self.O, self.PS
        acc = PS.get()
        for j in range(32):
            ap, tls = self.f32(r, j)
            sq = self.SQB.get()
            O.act(sq[:, 0:T], ap[:, 0:T], AF.Square, R=tls, W=[sq])
            C.pe([O.mm(acc[:, 0:T], self.ONESB[:, :], sq[:, 0:T], start=(j == 0), stop=(j == 31))], [sq], [acc])
            self.SQB.put(sq)
        O.act(self.RS[:, 0:T], acc[:, 0:T], AF.Ln, R=[acc], W=[self.RS], bias=self.EPS, scale=1.0 / 4096)
        PS.put(acc)
        O.act(self.RS[:, 0:T], self.RS[:, 0:T], AF.Exp, R=[self.RS], W=[self.RS], scale=-0.5)

    def prenorm(self, r, gname, q, T=TB):
        O = self.O
        self.rstd_of(r, T)
        for j in range(32):
            src, stl = self.f32(r, j)
            dst, dtl = self.bf(q, j)
            O.stt(dst[:, 0:T], src[:, 0:T], self.gain(gname, j), self.RS[:, 0:T], ALU.mult, ALU.mult, R=stl + [self.RS], W=dtl)

    def postnorm(self, r, gname, h_d, h_regs):
        C, O = self.C, self.O
        self.rstd_of(r)
        hjs = {}
        PRE = 4

        def issue(j):
            hj = self.STG.get()
            C.dma("sp", hj[:, :], h_d[j], reads=[h_regs[j]], writes=[hj])
            hjs[j] = hj

        for j in range(PRE):
            issue(j)
        for j in range(32):
            if j + PRE < 32:
                issue(j + PRE)
            src, stl = self.f32(r, j)
            hj = hjs.pop(j)
            O.stt(src, src, self.gain(gname, j), self.RS[:, :], ALU.mult, ALU.mult, R=stl + [self.RS], W=stl)
            O.tt(src, src, hj[:, :], ALU.add, R=stl + [hj], W=stl, eng="pool")
            self.STG.put(hj)
            C.dma("act", h_d[j], src, reads=stl, writes=[h_regs[j]])

    def wstream(self, jobs, pref=2):
        return WStream(self, jobs, pref)

    def chain(self, ws, ps, acts, T=TB, store_to=None):
        wt, KC = ws.next()
        if store_to is not None:
            self.C.dma("sp", store_to[0], wt[:, 0:KC, :], reads=[wt], writes=[store_to[1]])
        self.C.pe_steps([(self.O.mm(ps[:, 0:T], wt[:, kc, :], acts[kc][0][:, 0:T], start=(kc == 0), stop=(kc == KC - 1)), [wt] + list(acts[kc][1]))
                         for kc in range(KC)], [ps])
        self.WB.put(wt)


class WStream:
    def __init__(self, big, jobs, pref):
        self.b = big
        self.jobs = jobs
        self.pref = pref
        self.i = 0
        self.issued = 0
        self.q = []

    def _issue(self):
        job = self.jobs[self.issued]
        w_ap, KC = job[0], job[1]
        regs = job[2] if len(job) > 2 else []
        wt = self.b.WB.get()
        self.b.C.dma("pool", wt[:, 0:KC, :], w_ap, reads=regs, writes=[wt])
        self.q.append((wt, KC))
        self.issued += 1

    def next(self):
        while self.issued < min(self.i + 1 + self.pref, len(self.jobs)):
            self._issue()
        self.i += 1
        return self.q.pop(0)


def phase_A1(prog, cst_d, gains_d, x_d, NTOK, w_d, pT, pT_regs, w16=None):
    B = Big(prog, cst_d, gains_d)
    C, O, PS = B.C, B.O, B.PS
    z = B.STG.get()
    O.memset(z[:, :], 0.0, W=[z])
    for j in range(NCHA):
        C.dma("sp", pT[j, :, 0:4], z[:, 0:4], reads=[z], writes=[pT_regs[j]])
    B.STG.put(z)
    ws_all = B.wstream([(w_d[j], 32) for _ in range(NTOK // TB) for j in range(NCHA)])
    for s in range(NTOK // TB):
        t0 = s * TB
        B.load_T(x_d[t0:t0 + TB, :], TB // 128, 0)
        B.prenorm(0, "mix_norm_pre", 2)
        acts = [B.bf(2, kc) for kc in range(32)]
        if w16 is None:
            ws = ws_all
        elif s == 0:
            ws = B.wstream([(w_d[j], 32) for j in range(NCHA)])
        else:
            ws = B.wstream([(w16[0][j], 32, [w16[1][j]]) for j in range(NCHA)])
        for j in range(NCHA):
            ps = PS.get()
            B.chain(ws, ps, acts, store_to=((w16[0][j], w16[1][j]) if (w16 is not None and s == 0) else None))
            st = B.STG.get()
            O.copy(st[:, :], ps[:, :], R=[ps], W=[st], eng=("act" if j % 2 else "dve"))
            PS.put(ps)
            C.dma("sp", pT[j, :, 4 + t0:4 + t0 + TB], st[:, :], reads=[st], writes=[pT_regs[j]])
            B.STG.put(st)
    B.close()


def phase_B(prog, cst_d, gains_d, sel_d, x_d, mem_d, gat_blocks, gat_regs, NTB, W, h_d, out_d):
    B = Big(prog, cst_d, gains_d)
    nc, C, O, PS = B.nc, B.C, B.O, B.PS
    h_regs = [Tl(None, "h%d" % j) for j in range(32)]
    QT = Tl(sb(nc, "qT", [128, 4, 512], BF16), "qT")
    OT = Tl(sb(nc, "oT", [128, 4, 512], BF16), "oT")
    KT = Tl(sb(nc, "kT", [128, 4, 256], BF16), "kT")
    VT = Tl(sb(nc, "vT", [128, 2, 512], BF16), "vT")
    PRP = Pool(nc, "prp", 3, [128, 256], F32)
    PRB = Pool(nc, "prb", 3, [128, 256], BF16)
    PTB = Pool(nc, "ptb", 3, [128, 2, 128], BF16)
    SMB = Pool(nc, "smb", 4, [128, 4], F32)
    SCALE = float(128 ** -0.5)
    WR = W.get("_regs")

    def wj(k, j, KC):
        return (W[k][j], KC, [WR[k][j]]) if WR else (W[k][j], KC)

    YC = Pool(nc, "yc", 3, [128, 1024], BF16)
    SEL = Tl(sb(nc, "sel", [128, 4], F32), "sel")
    C.dma("sp", SEL[:, :], sel_d, writes=[SEL])
    C.barrier()

    B.load_T(mem_d, 2, 0)
    B.prenorm(0, "xa_norm_mem", 2, T=256)
    macts = [B.bf(2, kc) for kc in range(32)]
    ws = B.wstream([wj("w_k", j, 32) for j in range(4)])
    for h in range(4):
        ps = PS.get()
        B.chain(ws, ps, macts, T=256)
        O.copy(KT[:, h, :], ps[:, 0:256], R=[ps], W=[KT], eng="act")
        PS.put(ps)
    C.dma("pool", B.ar[:, 0:16384].rearrange("p (k n) -> p k n", n=512), W["w_v"], reads=(WR["w_v"] if WR else []), writes=B.g[0:32])
    for mt in range(2):
        ps = PS.get()
        C.pe([O.mm(ps[:, :], macts[kc][0][:, mt * 128:(mt + 1) * 128], B.bfall[0][:, kc, :], start=(kc == 0), stop=(kc == 31)) for kc in range(32)],
             B.g[0:32] + B.g[64:96], [ps])
        O.copy(VT[:, mt, :], ps[:, :], R=[ps], W=[VT], eng="dve")
        PS.put(ps)

    for s in range(NTB // TB):
        t0 = s * TB
        for r in range(4):
            for i in range(TB // 128):
                ys = B.YS.get()
                for k in range(4):
                    cand = YC.get()
                    blk = (k * NTB + s * TB) // YBLK
                    row = r * YBLK + i * 128
                    C.dma("sp", cand[:, :], gat_blocks[blk][row:row + 128, :], reads=[gat_regs[blk]], writes=[cand])
                    if k == 0:
                        O.ts(ys[:, :], cand[:, :], SEL[:, 0:1], None, ALU.mult, R=[cand], W=[ys])
                    else:
                        O.stt(ys[:, :], cand[:, :], SEL[:, k:k + 1], ys[:, :], ALU.mult, ALU.add, R=[cand, ys], W=[ys])
                    YC.put(cand)
                pb = PS.get()
                pbv = pb.t[:, :].bitcast(BF16)
                C.pe([O.tr(pbv[:, k * 128:(k + 1) * 128], ys[:, k * 128:(k + 1) * 128], B.IDB[:, :]) for k in range(8)], [ys], [pb])
                for part in range(2):
                    kc0 = part * 16 + r * 4
                    O.copy(B.bfall[3][:, kc0:kc0 + 4, i * 128:(i + 1) * 128], pbv[:, part * 512:(part + 1) * 512].rearrange("p (c t) -> p c t", t=128),
                           R=[pb], W=B.g[96 + kc0:96 + kc0 + 4], eng="dve")
                PS.put(pb)
                B.YS.put(ys)
        B.load_T(x_d[t0:t0 + TB, :], TB // 128, 0)
        for j in range(32):
            src, stl = B.f32(0, j)
            C.dma("sp", h_d[j], src, reads=stl, writes=[h_regs[j]])
        B.prenorm(0, "mix_norm_pre", 2)
        uacts = [B.bf(2, kc) for kc in range(32)]
        yrw = [B.bf(3, kc) for kc in range(16)]
        ygd = [B.bf(3, 16 + kc) for kc in range(16)]
        jobs = []
        for j in range(32):
            jobs += [wj("w_gate", j, 32), wj("w_gate", 32 + j, 32), wj("w_br_rw", j, 16), wj("w_br_gd", j, 16)]
        jobs += [wj("w_out", j, 32) for j in range(32)]
        jobs += [wj("w_q", j, 32) for j in range(4)]
        jobs += [wj("w_o", j, 4) for j in range(32)]
        for bi in range(8):
            jobs += [wj("w_up", bi * 16 + jj, 32) for jj in range(16)]
            jobs += [wj("w_down", bi * 32 + j, 16) for j in range(32)]
        ws = B.wstream(jobs)
        for j in range(32):
            p1, p2, p3, p4 = PS.get(), PS.get(), PS.get(), PS.get()
            B.chain(ws, p1, uacts)
            B.chain(ws, p2, uacts)
            B.chain(ws, p3, yrw)
            B.chain(ws, p4, ygd)
            s1, s2 = B.STG.get(), B.STG.get()
            O.act(s1[:, :], p1[:, :], AF.Sigmoid, R=[p1], W=[s1])
            O.act(s2[:, :], p2[:, :], AF.Sigmoid, R=[p2], W=[s2])
            O.tt(s1[:, :], s1[:, :], p3[:, :], ALU.mult, R=[s1, p3], W=[s1])
            O.tt(s2[:, :], s2[:, :], p4[:, :], ALU.mult, R=[s2, p4], W=[s2])
            dst, dtl = B.bf(0, j)
            O.tt(dst, s1[:, :], s2[:, :], ALU.add, R=[s1, s2], W=dtl, eng="pool")
            PS.put(p1, p2, p3, p4)
            B.STG.put(s1, s2)
        macts2 = [B.bf(0, kc) for kc in range(32)]
        for j in range(32):
            ps = PS.get()
            B.chain(ws, ps, macts2)
            dst, dtl = B.f32(1, j)
            O.copy(dst, ps[:, :], R=[ps], W=dtl, eng=("act" if j % 2 else "dve"))
            PS.put(ps)
        B.postnorm(1, "mix_norm_post", h_d, h_regs)
        B.prenorm(1, "xa_norm_pre", 0)
        cacts = [B.bf(0, kc) for kc in range(32)]
        for h in range(4):
            ps = PS.get()
            B.chain(ws, ps, cacts)
            O.copy(QT[:, h, :], ps[:, :], R=[ps], W=[QT], eng="act")
            PS.put(ps)
        for i in range(TB // 128):
            isl = slice(i * 128, (i + 1) * 128)
            for h in range(4):
                ps = PS.get()
                C.pe([O.mm(ps[:, 0:256], QT[:, h, isl], KT[:, h, :])], [QT, KT], [ps])
                sm = SMB.get()
                C.op("dve", lambda: nc.vector.tensor_reduce(out=sm[:, 0:1], in_=ps[:, 0:256], axis=AX.X, op=ALU.max), [ps], [sm])
                O.ts(sm[:, 1:2], sm[:, 0:1], -SCALE, None, ALU.mult, R=[sm], W=[sm])
                pr = PRP.get()
                O.act(pr[:, :], ps[:, 0:256], AF.Exp, R=[ps, sm], W=[pr, sm], bias=sm[:, 1:2], scale=SCALE, accum=sm[:, 2:3])
                PS.put(ps)
                O.recip(sm[:, 3:4], sm[:, 2:3], R=[sm], W=[sm])
                prb = PRB.get()
                O.ts(prb[:, :], pr[:, :], sm[:, 3:4], None, ALU.mult, R=[pr, sm], W=[prb])
                PRP.put(pr)
                SMB.put(sm)
                pb = PS.get()
                pbv = pb.t[:, :].bitcast(BF16)
                C.pe([O.tr(pbv[:, k * 128:(k + 1) * 128], prb[:, k * 128:(k + 1) * 128], B.IDB[:, :]) for k in range(2)], [prb], [pb])
                PRB.put(prb)
                pt = PTB.get()
                O.copy(pt[:, :, :], pbv[:, 0:256].rearrange("p (c t) -> p c t", t=128), R=[pb], W=[pt], eng="dve")
                PS.put(pb)
                po = PS.get()
                C.pe([O.mm(po[:, 0:128], VT[:, mc, h * 128:(h + 1) * 128], pt[:, mc, :], start=(mc == 0), stop=(mc == 1)) for mc in range(2)], [VT, pt], [po])
                PTB.put(pt)
                O.copy(OT[:, h, isl], po[:, 0:128], R=[po], W=[OT], eng="dve")
                PS.put(po)
        oacts = [(OT[:, h, :], [OT]) for h in range(4)]
        for j in range(32):
            ps = PS.get()
            B.chain(ws, ps, oacts)
            dst, dtl = B.f32(0, j)
            O.copy(dst, ps[:, :], R=[ps], W=dtl, eng=("act" if j % 2 else "dve"))
            PS.put(ps)
        B.postnorm(0, "xa_norm_post", h_d, h_regs)
        B.prenorm(0, "mlp_norm_pre", 2)
        facts = [B.bf(2, kc) for kc in range(32)]
        hacts = [B.bf(3, kc) for kc in range(16)]
        for bi in range(8):
            for jj in range(16):
                ps = PS.get()
                B.chain(ws, ps, facts)
                st = B.STG.get()
                O.act(st[:, :], ps[:, :], AF.Relu, R=[ps], W=[st])
                PS.put(ps)
                dst, dtl = hacts[jj]
                O.tt(dst, st[:, :], st[:, :], ALU.mult, R=[st], W=dtl, eng=("pool" if jj % 2 else "dve"))
                B.STG.put(st)
            for j in range(32):
                ps = PS.get()
                B.chain(ws, ps, hacts)
                dst, dtl = B.f32(0, j)
                if bi == 0:
                    O.copy(dst, ps[:, :], R=[ps], W=dtl, eng="act")
                else:
                    O.tt(dst, dst, ps[:, :], ALU.add, R=dtl + [ps], W=dtl)
                PS.put(ps)
        B.postnorm(0, "mlp_norm_post", h_d, h_regs)
        for i in range(TB // 128):
            for cp in range(4):
                xs = B.XS.get()
                for half in range(2):
                    ps = PS.get()
                    kc0 = cp * 8 + half * 4
                    tls = B.g[2 * kc0:2 * kc0 + 8]
                    C.pe([O.tr(ps[:, k * 128:(k + 1) * 128], B.f32all[0][:, kc0 + k, i * 128:(i + 1) * 128], B.IDENT) for k in range(4)], tls, [ps])
                    O.copy(xs[:, half * 512:(half + 1) * 512], ps[:, :], R=[ps], W=[xs], eng=("act" if half else "dve"))
                    PS.put(ps)
                row = s * TB + i * 128
                C.dma("sp", out_d[row:row + 128, cp * 1024:(cp + 1) * 1024], xs[:, :], reads=[xs])
                B.XS.put(xs)
    B.close()


NSEQ = 4096
NCORE = 8
PRECAST = False
GATE0 = 6592 + 8224

W_SHAPES = {
    "w_gate": [64, 128, 32, 128], "w_br_rw": [32, 128, 16, 128], "w_br_gd": [32, 128, 16, 128], "w_out": [32, 128, 32, 128],
    "w_q": [4, 128, 32, 128], "w_k": [4, 128, 32, 128], "w_v": [128, 32, 512], "w_o": [32, 128, 4, 128],
    "w_up": [128, 128, 32, 128], "w_down": [256, 128, 16, 128],
}


def build_program(nseq=NSEQ, phases="A1,A2,X,B", ydt=BF16):
    nc = bass.Bass("TRN2", target_bir_lowering=False)
    ntb = nseq // 4
    d = {}
    d["xb"] = nc.dram_tensor("xb", [nseq, 4096], F32, kind="ExternalInput").ap()
    d["xB"] = nc.dram_tensor("xB", [ntb, 4096], F32, kind="ExternalInput").ap()
    d["memb"] = nc.dram_tensor("memb", [256, 4096], F32, kind="ExternalInput").ap()
    d["cst"] = nc.dram_tensor("cst", [128, CST_N], F32, kind="ExternalInput").ap()
    d["gains"] = nc.dram_tensor("gains", [128, 7, 32], F32, kind="ExternalInput").ap()
    d["sel"] = nc.dram_tensor("sel", [128, 4], F32, kind="ExternalInput").ap()
    d["prm"] = nc.dram_tensor("prm", [128, PRM_N], F32, kind="ExternalInput").ap()
    d["bc"] = nc.dram_tensor("bc", [64, 3, 512], F32, kind="ExternalInput").ap()
    d["lora"] = nc.dram_tensor("lora", [128, 4, 512], F32, kind="ExternalInput").ap()
    d["w_inA"] = nc.dram_tensor("w_inA", [NCHA, 128, 32, 128], F32, kind="ExternalInput").ap()
    W = {}
    if "B" in phases:
        for k, shp in W_SHAPES.items():
            W[k] = nc.dram_tensor(k, shp, F32, kind="ExternalInput").ap()
    out = nc.dram_tensor("out", [ntb, 4096], F32, kind="ExternalOutput").ap()
    if "B" not in phases:
        dbg = nc.dram_tensor("dbg", [128, 1024], ydt, kind="ExternalOutput").ap()
    pT = nc.dram_tensor("pT_scr", [NCHA, 128, 4 + nseq], F32).ap()
    nblk = nseq // YBLK
    y_ts = [nc.dram_tensor("y_scr%d" % k, [YBLK, 1024], ydt) for k in range(nblk)]
    gat_ts = [nc.dram_tensor("gat_scr%d" % k, [4 * YBLK, 1024], ydt) for k in range(nblk)]
    y_regs = [Tl(None, "y%d" % k) for k in range(nblk)]
    gat_regs = [Tl(None, "gat%d" % k) for k in range(nblk)]
    h_d = nc.dram_tensor("h_scr", [32, 128, TB], F32).ap()
    P = Prog(nc)
    C = P.C
    pT_regs = [Tl(None, "pT%d" % j) for j in range(NCHA)]
    wA16 = (nc.dram_tensor("w_inA_b16", [NCHA, 128, 32, 128], BF16).ap(), [Tl(None, "wA%d" % j) for j in range(NCHA)])
    cast_jobs = []
    Wb = W
    if "B" in phases and PRECAST:
        Wb = {"_regs": {}}
        for k, shp in W_SHAPES.items():
            Wb[k] = nc.dram_tensor(k + "_b16", shp, BF16).ap()
            if k == "w_v":
                rs_ = [Tl(None, "w_v_r%d" % i) for i in range(4)]
                Wb["_regs"][k] = rs_
                cast_jobs += [(Wb[k][:, 8 * i:8 * (i + 1), :], W[k][:, 8 * i:8 * (i + 1), :], rs_[i]) for i in range(4)]
            else:
                Wb["_regs"][k] = [Tl(None, "%s_r%d" % (k, j)) for j in range(shp[0])]
                cast_jobs += [(Wb[k][j], W[k][j], Wb["_regs"][k][j]) for j in range(shp[0])]
    cast_state = {"i": 0}

    def cast_some(s, npass):
        n = len(cast_jobs)
        upto = n if s + 1 >= npass else (n * (s + 1)) // npass
        while cast_state["i"] < upto:
            dst, src, reg = cast_jobs[cast_state["i"]]
            C.dma("pool", dst, src, writes=[reg])
            cast_state["i"] += 1

    if "A1" in phases:
        phase_A1(P, d["cst"], d["gains"], d["xb"], nseq, d["w_inA"], pT, pT_regs, w16=(wA16 if PRECAST else None))
    if "A2" in phases:
        mark = (nc.sbuf_base, nc.sbuf_top)
        P.load_consts(d["cst"])
        state = {"prev": None}

        def exchange(k):
            C.deps("pool", [], [y_regs[k]])
            if state["prev"] is not None:
                C._wait("pool", state["prev"])
            key = ("cc", k)
            cc = nc.gpsimd.collective_compute("AllGather", ALU.bypass, replica_groups=[[0, 1, 2, 3], [4, 5, 6, 7]],
                                              ins=[y_ts[k].ap().opt()], outs=[gat_ts[k].ap().opt()])
            cc.then_inc(C._sem(key))
            ev = Ev(key, 1)
            gat_regs[k].w = ev
            state["prev"] = ev

        P.phase_A2(nseq, pT, pT_regs, d["prm"], d["bc"], d["lora"], [t.ap() for t in y_ts], y_regs,
                   after_block=(exchange if "X" in phases else None), ydt=ydt, after_pass=cast_some)
        nc.sbuf_base, nc.sbuf_top = mark
    if "B" in phases:
        cast_some(0, 1)
        phase_B(P, d["cst"], d["gains"], d["sel"], d["xB"], d["memb"], [t.ap() for t in gat_ts], gat_regs, ntb, Wb, h_d, out)
    if "B" not in phases:
        if "X" in phases:
            C.dma("sp", dbg, gat_ts[nblk - 1].ap()[3 * YBLK:3 * YBLK + 128, :], reads=[gat_regs[nblk - 1]])
        else:
            C.dma("sp", dbg, y_ts[0].ap()[0:128, :])
    C.barrier()
    return nc, P


_HOST_CACHE = {}


def prepare_inputs(inp, nseq=NSEQ):
    f = lambda a: np.ascontiguousarray(np.asarray(a, dtype=np.float32))
    w_in = np.asarray(inp["w_in"], dtype=np.float32)[0]
    shared = {
        "cst": make_consts(),
        "gains": np.ascontiguousarray(np.stack([np.asarray(inp[n], np.float32)[0].reshape(32, 128).T for n in GAIN_NAMES], axis=1)),
        "w_gate": arr_w(w_in[:, GATE0:GATE0 + 8192]),
        "w_br_rw": arr_w(f(inp["w_branch_rwkv"])[0]),
        "w_br_gd": arr_w(f(inp["w_branch_gdn"])[0]),
        "w_out": arr_w(f(inp["w_mix_out"])[0]),
        "w_q": arr_w(f(inp["xa_w_q"])[0]),
        "w_k": arr_w(f(inp["xa_w_kv"])[0][:, 0:512]),
        "w_v": np.ascontiguousarray(f(inp["xa_w_kv"])[0][:, 512:1024].reshape(32, 128, 512).transpose(1, 0, 2)),
        "w_o": arr_w(f(inp["xa_w_o"])[0]),
        "w_up": arr_w(f(inp["mlp_w_up"])[0]),
        "w_down": np.concatenate([arr_w(f(inp["mlp_w_down"])[0][bi * 2048:(bi + 1) * 2048, :]) for bi in range(8)], axis=0),
    }
    x = np.asarray(inp["x"], np.float32)
    mem = np.asarray(inp["mem"], np.float32)
    ntb = nseq // 4
    groups = {}
    for g in range(4):
        idx = mixer_col_index(g)
        wA = np.zeros((4096, NCHA * 128), np.float32)
        m = idx >= 0
        wA[:, m] = w_in[:, idx[m]]
        prm, bc, lora = pack_mixer_params(inp, g)
        sel = np.zeros((128, 4), np.float32)
        sel[:, g] = 1.0
        groups[g] = {"w_inA": arr_w(wA), "prm": prm, "bc": bc, "lora": lora, "sel": sel}
    in_maps = []
    for c in range(NCORE):
        b, g = c // 4, c % 4
        m_ = dict(shared)
        m_.update(groups[g])
        m_["xb"] = np.ascontiguousarray(x[b, 0:nseq])
        m_["xB"] = np.ascontiguousarray(x[b, g * ntb:(g + 1) * ntb])
        m_["memb"] = np.ascontiguousarray(mem[b])
        in_maps.append(m_)
    return in_maps


def kernel(**inputs):
    nc, _ = build_program()
    in_maps = prepare_inputs(inputs)
    res = run_bass_kernel_spmd(nc, in_maps, core_ids=list(range(NCORE)))
    out = np.zeros((2, NSEQ, 4096), np.float32)
    ntb = NSEQ // 4
    for c in range(NCORE):
        b, g = c // 4, c % 4
        out[b, g * ntb:(g + 1) * ntb] = np.asarray(res.results[c]["out"], np.float32)
    return out
```
